# Optimizing a Trainium2 kernel written in Bass

```python
import math
import jax, jax.numpy as jnp
from jax import lax
import numpy as np

D_MODEL = 1024
BATCH = 8
SEQ = 8192
DEPTH = 2
DEC_BATCH = 4
DEC_SEQ = 8192
PAST_LEN = 128

D_MIX = D_MODEL
RW_HEADS = 8
RW_HEAD = 64
RW_W = RW_HEADS * RW_HEAD
LORA_W = 64
LORA_A = 64
LORA_G = 128
RW_SPLITS = [RW_W, 2 * RW_W, 3 * RW_W, 3 * RW_W + 2 * LORA_W, 3 * RW_W + 2 * LORA_W + 2 * LORA_A]
RW_COLS = 3 * RW_W + 2 * LORA_W + 2 * LORA_A + LORA_G
DA_HEADS = 4
DA_HEAD = 64
DA_W = DA_HEADS * 2 * DA_HEAD
DA_COLS = 3 * DA_W
N_IN = RW_COLS + DA_COLS
D_FF = 2816
ROPE_THETA = 10000.0
Q_BLOCK = 128
LN_EPS = 1e-5
GN_EPS = 64e-5
SUBLN_EPS = 1e-5
ALPHA = (2.0 * DEPTH) ** 0.25
BETA = (8.0 * DEPTH) ** -0.25

kernel_name = "hymba_rwkv7_diffattn_macaron_deepnorm_encoder"


def layer_norm(x, g, b):
    xf = x.astype(jnp.float32)
    mu = jnp.mean(xf, axis=-1, keepdims=True)
    var = jnp.mean(jnp.square(xf - mu), axis=-1, keepdims=True)
    return ((xf - mu) * lax.rsqrt(var + LN_EPS) * g + b).astype(x.dtype)


def swiglu(x, w_gu, w_down):
    gate, up = jnp.split(x @ w_gu, 2, axis=-1)
    return (jax.nn.silu(gate) * up) @ w_down


def centred_token_shift(p, mu_prev, mu_next):
    prev = jnp.pad(p[:, :-1], ((0, 0), (1, 0), (0, 0)))
    nxt = jnp.pad(p[:, 1:], ((0, 0), (0, 1), (0, 0)))
    return p + mu_prev * (prev - p) + mu_next * (nxt - p)


def rwkv7_mixer(p, w_up, w_bias, a_up, a_bias, g_up, k_k, k_a, r_k, gn_g, gn_b):
    B, T, _ = p.shape
    H, N, C = RW_HEADS, RW_HEAD, RW_W
    f32 = jnp.float32
    r, k, v, wd, ad, gd = jnp.split(p, RW_SPLITS, axis=-1)
    wd = wd.reshape(B, T, 2, LORA_W)
    ad = ad.reshape(B, T, 2, LORA_A)
    w_logit = (w_bias + jnp.einsum('btdr,drc->btdc', jnp.tanh(wd), w_up)).astype(f32)
    decay = jnp.exp(-jnp.exp(-jax.nn.softplus(-w_logit) - 0.5))
    a = jax.nn.sigmoid((a_bias + jnp.einsum('btdr,drc->btdc', ad, a_up)).astype(f32))
    g = (jax.nn.sigmoid(gd) @ g_up).astype(f32)
    r = r.astype(f32)
    k = k.astype(f32)
    v = v.astype(f32)
    kk = (k * k_k).reshape(B, T, H, N)
    kk = kk * lax.rsqrt(jnp.maximum(jnp.sum(kk * kk, axis=-1, keepdims=True), 1e-24))
    kk = kk.reshape(B, T, C)
    k_mod = k[:, :, None] * (1.0 + (a - 1.0) * k_a)

    def dirs(z):
        z = jnp.stack([z[:, :, 0], jnp.flip(z[:, :, 1], axis=1)], axis=1)
        return z.reshape(B, 2, T, H, N).transpose(2, 1, 0, 3, 4)

    def both(z):
        return dirs(jnp.stack([z, z], axis=2))

    def step(S, inp):
        r_t, w_t, k_t, v_t, kk_t, a_t = inp
        sa = jnp.einsum('dbhij,dbhj->dbhi', S, -kk_t)
        S = (S * w_t[..., None, :] + sa[..., :, None] * (kk_t * a_t)[..., None, :]
             + v_t[..., :, None] * k_t[..., None, :])
        return S, jnp.einsum('dbhij,dbhj->dbhi', S, r_t)

    S0 = jnp.zeros((2, B, H, N, N), f32)
    _, o = lax.scan(step, S0, (both(r), dirs(decay), dirs(k_mod), both(v), both(kk), dirs(a)))
    o = o[:, 0] + jnp.flip(o[:, 1], axis=0)
    o = o.transpose(1, 0, 2, 3)
    mu = jnp.mean(o, axis=-1, keepdims=True)
    var = jnp.mean(jnp.square(o - mu), axis=-1, keepdims=True)
    o = ((o - mu) * lax.rsqrt(var + GN_EPS)).reshape(B, T, C) * gn_g + gn_b
    rk = jnp.sum((r * jnp.sum(k_mod, axis=2)).reshape(B, T, H, N) * r_k, axis=-1, keepdims=True)
    o = o + (rk * v.reshape(B, T, H, N)).reshape(B, T, C)
    return (o * g).astype(p.dtype)


def rope_tables(T, n):
    inv = jnp.power(ROPE_THETA, -jnp.arange(0, n, 2, dtype=jnp.float32) / n)
    ang = jnp.arange(T, dtype=jnp.float32)[:, None] * inv[None, :]
    ang = jnp.concatenate([ang, ang], axis=-1)
    return jnp.cos(ang), jnp.sin(ang)


def apply_rope(x, cos, sin):
    x1, x2 = jnp.split(x, 2, axis=-1)
    rot = jnp.concatenate([-x2, x1], axis=-1)
    return (x * cos + rot * sin).astype(x.dtype)


def diff_attention(p, lam, lam_init, subln_g):
    B, T, _ = p.shape
    H, N = DA_HEADS, DA_HEAD
    f32 = jnp.float32
    q, k, v = jnp.split(p, 3, axis=-1)
    q = q.reshape(B, T, H, 2, N)
    k = k.reshape(B, T, H, 2, N)
    v = v.reshape(B, T, H, 2 * N)
    cos, sin = rope_tables(T, N)
    cos = cos[:, None, None, :].astype(p.dtype)
    sin = sin[:, None, None, :].astype(p.dtype)
    q = apply_rope(q, cos, sin)
    k = apply_rope(k, cos, sin)
    scale = 1.0 / math.sqrt(N)
    nb = T // Q_BLOCK
    qb = q.reshape(B, nb, Q_BLOCK, H, 2, N).swapaxes(0, 1)

    def attend(q_blk):
        s = jnp.einsum('bqhcn,bkhcn->bhcqk', q_blk, k).astype(f32) * scale
        pr = jax.nn.softmax(s, axis=-1)
        pr = pr[:, :, 0] - lam * pr[:, :, 1]
        return jnp.einsum('bhqk,bkhv->bqhv', pr.astype(v.dtype), v)

    o = lax.map(attend, qb).swapaxes(0, 1).reshape(B, T, H, 2 * N).astype(f32)
    o = o * lax.rsqrt(jnp.mean(o * o, axis=-1, keepdims=True) + SUBLN_EPS) * subln_g
    o = o * (1.0 - lam_init)
    return o.reshape(B, T, DA_W).astype(p.dtype)


def encoder_layer(x, lam_init, ln_g, ln_b, ffn_w_gu, ffn_w_down, w_in, w_out, shift_mu,
                  rw_w_up, rw_w_bias, rw_a_up, rw_a_bias, rw_g_up, rw_k_k, rw_k_a, rw_r_k,
                  rw_gn_g, rw_gn_b, da_lambda, da_subln_g):
    x = layer_norm(ALPHA * x + 0.5 * swiglu(x, ffn_w_gu[0], ffn_w_down[0]), ln_g[0], ln_b[0])
    proj = x @ w_in
    p_rw = centred_token_shift(proj[..., :RW_COLS], shift_mu[0], shift_mu[1])
    p_da = proj[..., RW_COLS:]
    y_rw = rwkv7_mixer(p_rw, rw_w_up, rw_w_bias, rw_a_up, rw_a_bias, rw_g_up,
                       rw_k_k, rw_k_a, rw_r_k, rw_gn_g, rw_gn_b)
    lq1, lk1, lq2, lk2 = (da_lambda[i].astype(jnp.float32) for i in range(4))
    lam = jnp.exp(jnp.sum(lq1 * lk1)) - jnp.exp(jnp.sum(lq2 * lk2)) + lam_init
    y_da = diff_attention(p_da, lam, lam_init, da_subln_g)
    h = jnp.concatenate([y_rw, y_da], axis=-1) @ w_out
    x = layer_norm(ALPHA * x + h, ln_g[1], ln_b[1])
    x = layer_norm(ALPHA * x + 0.5 * swiglu(x, ffn_w_gu[1], ffn_w_down[1]), ln_g[2], ln_b[2])
    return x


def setup_inputs(seed: int = 0) -> dict:
    key = jax.random.key(seed)
    ks = jax.random.split(key, 21)
    f32 = jnp.float32

    def nrm(k, shape, s):
        return jax.random.normal(k, shape, f32) * s

    return {
        "x_prompt": nrm(ks[0], (BATCH, SEQ, D_MODEL), 1.0),
        "x_sample": nrm(ks[1], (DEC_BATCH, DEC_SEQ, D_MODEL), 1.0),
        "ln_g": 1.0 + nrm(ks[2], (DEPTH, 3, D_MODEL), 0.02),
        "ln_b": nrm(ks[3], (DEPTH, 3, D_MODEL), 0.02),
        "ffn_w_gu": nrm(ks[4], (DEPTH, 2, D_MODEL, 2 * D_FF), D_MODEL ** -0.5),
        "ffn_w_down": nrm(ks[5], (DEPTH, 2, D_FF, D_MODEL), BETA * D_FF ** -0.5),
        "w_in": nrm(ks[6], (DEPTH, D_MODEL, N_IN), D_MODEL ** -0.5),
        "w_out": nrm(ks[7], (DEPTH, D_MIX, D_MODEL), BETA * D_MIX ** -0.5),
        "shift_mu": jax.random.uniform(ks[8], (DEPTH, 2, RW_COLS), f32, 0.0, 0.5),
        "rw_w_up": nrm(ks[9], (DEPTH, 2, LORA_W, RW_W), 0.1),
        "rw_w_bias": jax.random.uniform(ks[10], (DEPTH, 2, RW_W), f32, -6.0, 1.0),
        "rw_a_up": nrm(ks[11], (DEPTH, 2, LORA_A, RW_W), 0.1),
        "rw_a_bias": nrm(ks[12], (DEPTH, 2, RW_W), 0.1),
        "rw_g_up": nrm(ks[13], (DEPTH, LORA_G, RW_W), LORA_G ** -0.5),
        "rw_k_k": 0.85 + nrm(ks[14], (DEPTH, RW_W), 0.02),
        "rw_k_a": 1.0 + nrm(ks[15], (DEPTH, RW_W), 0.02),
        "rw_r_k": nrm(ks[16], (DEPTH, RW_HEADS, RW_HEAD), 0.1),
        "rw_gn_g": 1.0 + nrm(ks[17], (DEPTH, RW_W), 0.02),
        "rw_gn_b": nrm(ks[18], (DEPTH, RW_W), 0.02),
        "da_lambda": nrm(ks[19], (DEPTH, 4, DA_HEAD), 0.1),
        "da_subln_g": 1.0 + nrm(ks[20], (DEPTH, 2 * DA_HEAD), 0.02),
    }


def reference(x_prompt, x_sample, ln_g, ln_b, ffn_w_gu, ffn_w_down, w_in, w_out, shift_mu,
              rw_w_up, rw_w_bias, rw_a_up, rw_a_bias, rw_g_up, rw_k_k, rw_k_a, rw_r_k,
              rw_gn_g, rw_gn_b, da_lambda, da_subln_g):
    def trunk(x):
        for l in range(DEPTH):
            lam_init = 0.8 - 0.6 * math.exp(-0.3 * l)
            x = encoder_layer(x, lam_init, ln_g[l], ln_b[l], ffn_w_gu[l], ffn_w_down[l],
                              w_in[l], w_out[l], shift_mu[l], rw_w_up[l], rw_w_bias[l],
                              rw_a_up[l], rw_a_bias[l], rw_g_up[l], rw_k_k[l], rw_k_a[l],
                              rw_r_k[l], rw_gn_g[l], rw_gn_b[l], da_lambda[l], da_subln_g[l])
        return x

    y_prompt = trunk(x_prompt)
    y_sample = trunk(x_sample)
    return (y_prompt, y_sample)
```

```python
import math
from contextlib import ExitStack

import numpy as np
import ml_dtypes

import concourse.bass as bass
import concourse.mybir as mybir
from concourse.bass_utils import run_bass_kernel_spmd

F32 = mybir.dt.float32
BF16 = mybir.dt.bfloat16
AF = mybir.ActivationFunctionType
ALU = mybir.AluOpType
AX = mybir.AxisListType

D_MODEL = 1024
DEPTH = 2
D_FF = 2816
RW_W = 512
RW_COLS = 1920
DA_W = 512
N_IN = 3456
LN_EPS = 1e-5
GN_EPS = 64e-5
SUBLN_EPS = 1e-5
ALPHA = (2.0 * DEPTH) ** 0.25
ROPE_THETA = 10000.0

SAME_ENGINE_SYNC = False
import os as _os
ATTACH_WAITS = int(_os.environ.get('ATTACH_WAITS', '1'))


class Buf:
    __slots__ = ("name", "writers", "readers", "multi")

    def __init__(self, name, multi=False):
        self.name = name
        self.writers = {}
        self.readers = {}
        self.multi = multi


class Ins:
    __slots__ = ("eng", "idx", "fn", "waits", "signal", "lane", "lane_cnt", "clock", "rank")


class Prog:
    ENGS = ("sp", "act", "dve", "pool", "pe")

    def __init__(self):
        self.streams = {e: [] for e in self.ENGS}
        self.seen = {e: {} for e in self.ENGS}
        self.lane_cnt = {}
        self.touched = {}
        self.lane_alias = {}
        self.global_reads = []
        self.barb = Buf("barrier")
        self.bar_ap = None

    def barrier(self):
        bufs = list(self.touched.values())
        self.touched = {}
        self.global_reads = []
        ap = self.bar_ap
        self.add("dve", lambda e: e.memset(ap, 0.0), writes=bufs + [self.barb])
        self.lane_alias = {}
        self.touched = {}
        self.global_reads = [self.barb]

    def add(self, eng, fn, reads=(), writes=(), lane=None, fs=False):
        ins = Ins()
        ins.eng = eng
        ins.idx = len(self.streams[eng])
        ins.fn = fn
        ins.signal = False
        ins.lane = lane
        ins.rank = None
        if lane is not None:
            if lane not in self.lane_alias:
                self.lane_alias[lane] = "P%d" % len(self.lane_alias)
            lane = self.lane_alias[lane]
            ins.lane = lane
            self.lane_cnt[lane] = self.lane_cnt.get(lane, 0) + 1
            ins.lane_cnt = self.lane_cnt[lane]
        else:
            ins.lane_cnt = 0
        if self.global_reads:
            reads = list(reads) + self.global_reads
        for b in reads:
            self.touched[id(b)] = b
        for b in writes:
            self.touched[id(b)] = b
        deps = []
        for b in reads:
            deps.extend(b.writers.values())
        for b in writes:
            if not (b.multi and not b.readers):
                deps.extend(b.writers.values())
            deps.extend(b.readers.values())
        seen = self.seen[eng]
        waits = []
        new_seen = None
        for p in deps:
            if p.lane is not None:
                key = ("L", p.lane)
                val = p.lane_cnt
            else:
                if p.eng == eng and lane is None and not (SAME_ENGINE_SYNC or fs):
                    continue
                key = ("E", p.eng)
                val = p.idx
            cur = (new_seen if new_seen is not None else seen).get(key, -1)
            if cur >= val:
                continue
            if new_seen is None:
                new_seen = dict(seen)
            waits.append(p)
            p.signal = True
            new_seen[key] = val
            for k2, v2 in p.clock.items():
                if new_seen.get(k2, -1) < v2:
                    new_seen[k2] = v2
        if new_seen is not None:
            best = {}
            for p in waits:
                key = ("L", p.lane) if p.lane is not None else ("E", p.eng)
                val = p.lane_cnt if p.lane is not None else p.idx
                if key not in best or best[key][0] < val:
                    best[key] = (val, p)
            waits = [v[1] for v in best.values()]
            self.seen[eng] = new_seen
            seen = new_seen
        ins.waits = waits
        ins.clock = seen
        self.streams[eng].append(ins)
        key = ("L", lane) if lane is not None else ("E", eng)
        for b in reads:
            b.readers[key] = ins
        for b in writes:
            if b.multi and not b.readers:
                b.writers[key] = ins
            else:
                b.writers = {key: ins}
                b.readers = {}
        return ins

    def emit(self, nc, es):
        sems = {}
        for e in ("act", "dve", "pool", "pe"):
            sems[("E", e)] = es.enter_context(nc.semaphore("sem_" + e))
        for ln in self.lane_cnt:
            sems[("L", ln)] = es.enter_context(nc.semaphore("lane_" + ln))
        for e in self.ENGS:
            r = 0
            for ins in self.streams[e]:
                if ins.lane is None and ins.signal:
                    r += 1
                    ins.rank = r
        block = es.enter_context(nc.Block())
        final_lanes = dict(self.lane_cnt)

        def run(engname, eobj):
            for ins in self.streams[engname]:
                wl = []
                for p in ins.waits:
                    if p.lane is not None:
                        wl.append((sems[("L", p.lane)], 16 * p.lane_cnt))
                    else:
                        wl.append((sems[("E", p.eng)], p.rank))
                att = ATTACH_WAITS and bool(wl) and (ins.lane is None or ATTACH_WAITS > 1)
                for sm, vl in (wl[1:] if att else wl):
                    eobj.wait_ge(sm, vl)
                bi = ins.fn(eobj)
                if att:
                    bi._wait_ge(wl[0][0], wl[0][1])
                if ins.lane is not None:
                    bi.then_inc(sems[("L", ins.lane)], 16)
                elif ins.signal:
                    bi.then_inc(sems[("E", ins.eng)], 1)
            if engname == "sp":
                for ln, c in final_lanes.items():
                    eobj.wait_ge(sems[("L", ln)], 16 * c)

        @block.sync
        def _(e):
            run("sp", e)

        @block.scalar
        def _(e):
            run("act", e)

        @block.vector
        def _(e):
            run("dve", e)

        @block.gpsimd
        def _(e):
            run("pool", e)

        @block.tensor
        def _(e):
            run("pe", e)


class Arena:
    def __init__(self, ap, words):
        self.ap = ap
        self.words = words
        self.pos = 0
        self.marks = []

    def push(self):
        self.marks.append(self.pos)

    def pop(self):
        self.pos = self.marks.pop()

    def f32(self, n, parts=128):
        a = self.pos
        self.pos += n
        assert self.pos <= self.words, ("SBUF arena overflow", self.pos, self.words)
        return self.ap[0:parts, a:a + n]

    def bf16(self, n, parts=128):
        w = (n + 1) // 2
        a = self.pos
        self.pos += w
        assert self.pos <= self.words, ("SBUF arena overflow", self.pos, self.words)
        return self.ap[0:parts, a:a + w].bitcast(BF16)[:, 0:n]


class Ctx:
    pass


TT = 256


def load_ln_params(C, P, ar, ln_g_ap, ln_b_ap, l, i, name):
    nc = C.nc
    g = ar.f32(8)
    b = ar.f32(8)
    bg = Buf(name + "_g")
    bb = Buf(name + "_b")
    P.add("sp", lambda e: e.dma_start(out=g, in_=ln_g_ap[l, i].rearrange("(c p) -> p c", p=128),
                                      allow_slow_non_contiguous=True), writes=[bg], lane=name + "_g")
    P.add("sp", lambda e: e.dma_start(out=b, in_=ln_b_ap[l, i].rearrange("(c p) -> p c", p=128),
                                      allow_slow_non_contiguous=True), writes=[bb], lane=name + "_b")
    return (g, bg), (b, bb)


def layer_norm_fm(C, P, z, zbuf, ntok, lng, lnb, out, outbuf, tmp, eps=LN_EPS):
    nc = C.nc
    (g, bg), (b, bb) = lng, lnb
    zb, zbb = tmp["zb"]
    zq, zqb = tmp["zq"]
    ps, psb = tmp["ps"]
    mr, mrb = tmp["mr"]
    P.add("act", lambda e: e.activation(out=zb, in_=z, func=AF.Copy), reads=[zbuf], writes=[zbb])
    P.add("dve", lambda e: e.tensor_tensor(out=zq, in0=z, in1=z, op=ALU.mult), reads=[zbuf], writes=[zqb])
    for d in range(8):
        P.add("pe", lambda e, d=d: e.matmul(ps[:, 0:ntok], lhsT=C.ones_bf, rhs=zb[:, d, :], start=(d == 0), stop=(d == 7)),
              reads=[zbb, C.constb], writes=[psb])
    for d in range(8):
        P.add("pe", lambda e, d=d: e.matmul(ps[:, 256:256 + ntok], lhsT=C.ones_bf, rhs=zq[:, d, :], start=(d == 0), stop=(d == 7)),
              reads=[zqb, C.constb], writes=[psb])
    mean = mr[:, 0, :]
    rstd = mr[:, 1, :]
    inv = 1.0 / D_MODEL
    P.add("dve", lambda e: e.tensor_scalar(out=mean, in0=ps[:, 0:ntok], scalar1=inv, scalar2=None, op0=ALU.mult),
          reads=[psb], writes=[mrb])
    P.add("dve", lambda e: e.tensor_tensor(out=rstd, in0=mean, in1=mean, op=ALU.mult), reads=[mrb], writes=[mrb])
    P.add("dve", lambda e: e.scalar_tensor_tensor(out=rstd, in0=ps[:, 256:256 + ntok], scalar=inv, in1=rstd,
                                                   op0=ALU.mult, op1=ALU.subtract), reads=[psb, mrb], writes=[mrb])
    P.add("act", lambda e: e.activation(out=rstd, in_=rstd, func=AF.Sqrt, bias=eps, scale=1.0), reads=[mrb], writes=[mrb])
    P.add("dve", lambda e: e.reciprocal(out=rstd, in_=rstd), reads=[mrb], writes=[mrb])
    mean_b = mean.unsqueeze(1).to_broadcast([128, 8, ntok])
    rstd_b = rstd.unsqueeze(1).to_broadcast([128, 8, ntok])
    P.add("dve", lambda e: e.tensor_tensor(out=z, in0=z, in1=mean_b, op=ALU.subtract), reads=[zbuf, mrb], writes=[zbuf])
    P.add("dve", lambda e: e.tensor_tensor(out=z, in0=z, in1=rstd_b, op=ALU.mult), reads=[zbuf, mrb], writes=[zbuf])
    for d in range(8):
        P.add("act", lambda e, d=d: e.activation(out=out[:, d, :], in_=z[:, d, :], func=AF.Identity,
                                                 scale=g[:, d:d + 1], bias=b[:, d:d + 1]),
              reads=[zbuf, bg, bb], writes=[outbuf])


def pass_ffn(C, P, ar, src, src_kind, dst, dst_kind, wgu_ap, wdn_ap, lng_ap, lnb_ap, l, i, ntok_total, tag):
    nc = C.nc
    ar.push()
    NF = D_FF // 128
    wgu = ar.bf16(8 * 2 * D_FF).rearrange("p (k n) -> p k n", k=8)
    wdn = ar.bf16(NF * D_MODEL).rearrange("p (k n) -> p k n", k=NF)
    wgub = Buf(tag + "wgu", multi=True)
    wdnb = Buf(tag + "wdn", multi=True)
    for k in range(8):
        P.add("pool", lambda e, k=k: e.dma_start(out=wgu[:, k, :], in_=wgu_ap[k * 128:(k + 1) * 128, :]),
              writes=[wgub], lane="wgu%d" % (k % 4))
    for k in range(NF):
        P.add("pool", lambda e, k=k: e.dma_start(out=wdn[:, k, :], in_=wdn_ap[k * 128:(k + 1) * 128, :]),
              writes=[wdnb], lane="wdn%d" % (k % 4))
    lng, lnb = load_ln_params(C, P, ar, lng_ap, lnb_ap, l, i, tag + "ln")
    ntiles = ntok_total // TT
    NB = 2
    xs = [ar.f32(8 * TT).rearrange("p (c t) -> p c t", c=8) for _ in range(NB)]
    xsb = [Buf(tag + "x%d" % j) for j in range(NB)]
    xtok = None
    if src_kind == "tok":
        xtok = [ar.f32(2 * D_MODEL).rearrange("p (j d) -> p j d", j=2) for _ in range(NB)]
        xtokb = [Buf(tag + "xtok%d" % j) for j in range(NB)]
    xb = ar.bf16(8 * TT).rearrange("p (c t) -> p c t", c=8)
    xbb = Buf(tag + "xb")
    h = ar.bf16(NF * TT).rearrange("p (c t) -> p c t", c=NF)
    hb = Buf(tag + "h")
    sg = [ar.f32(TT) for _ in range(2)]
    sgb = [Buf(tag + "sg%d" % j) for j in range(2)]
    zb = xb
    zq = h[:, 0:8, :]
    mr = ar.f32(2 * TT).rearrange("p (c t) -> p c t", c=2)
    tmp = {"zb": (zb, xbb), "zq": (zq, hb), "ps": (C.psum[4], C.psumb[4]),
           "mr": (mr, Buf(tag + "mr"))}
    if dst_kind == "tok":
        if xtok is not None:
            otok, otokb = xtok, xtokb
        else:
            otok = [ar.f32(2 * D_MODEL).rearrange("p (j d) -> p j d", j=2) for _ in range(2)]
            otokb = [Buf(tag + "otok%d" % j) for j in range(2)]
    src_buf, src_ap = src
    dst_buf, dst_ap = dst

    def stage_load(t):
        j = t % NB
        t0 = t * TT
        if src_kind == "fm":
            P.add("sp", lambda e: e.dma_start(out=xs[j], in_=src_ap.rearrange("(c p) t -> p c t", p=128)[:, :, t0:t0 + TT]),
                  reads=[src_buf], writes=[xsb[j]], lane=tag + "ldx%d" % j)
        else:
            P.add("sp", lambda e: e.dma_start(out=xtok[j], in_=src_ap[t0:t0 + TT, :].rearrange("(j p) d -> p j d", p=128)),
                  reads=[src_buf], writes=[xtokb[j]], lane=tag + "ldx%d" % j)
            for jj in range(2):
                for c in range(8):
                    pb = (jj * 8 + c) % 2
                    bank = C.psum[6 + pb][:, 0:128]
                    bankb = C.psumb[6 + pb]
                    P.add("pe", lambda e, jj=jj, c=c, bank=bank: e.transpose(bank[:, 0:128], xtok[j][:, jj, c * 128:(c + 1) * 128], C.ident_f32),
                          reads=[xtokb[j], C.constb], writes=[bankb])
                    P.add("dve", lambda e, jj=jj, c=c, bank=bank: e.tensor_copy(out=xs[j][:, c, jj * 128:(jj + 1) * 128], in_=bank[:, 0:128]),
                          reads=[bankb], writes=[xsb[j]])

    def stage_cast(t):
        j = t % NB
        P.add("act", lambda e: e.activation(out=xb, in_=xs[j], func=AF.Copy), reads=[xsb[j]], writes=[xbb])

    def stage_ffn(t):
        j = t % NB
        for f in range(NF):
            bank = C.psum[f % 4]
            bankb = C.psumb[f % 4]
            for half, col0 in ((0, f * 128), (1, D_FF + f * 128)):
                for k in range(8):
                    P.add("pe", lambda e, k=k, half=half, col0=col0, bank=bank: e.matmul(
                        bank[:, half * 256:half * 256 + TT], lhsT=wgu[:, k, col0:col0 + 128], rhs=xb[:, k, :],
                        start=(k == 0), stop=(k == 7)), reads=[wgub, xbb], writes=[bankb])
            s = sg[f % 2]
            sb = sgb[f % 2]
            P.add("act", lambda e, bank=bank, s=s: e.activation(out=s, in_=bank[:, 0:TT], func=AF.Silu),
                  reads=[bankb], writes=[sb])
            P.add("dve", lambda e, bank=bank, s=s, f=f: e.tensor_tensor(out=h[:, f, :], in0=s, in1=bank[:, 256:256 + TT], op=ALU.mult),
                  reads=[bankb, sb], writes=[hb])
        for d in range(8):
            bank = C.psum[4 + d % 2]
            bankb = C.psumb[4 + d % 2]
            for f in range(NF):
                P.add("pe", lambda e, f=f, d=d, bank=bank: e.matmul(bank[:, 0:TT], lhsT=wdn[:, f, d * 128:(d + 1) * 128], rhs=h[:, f, :],
                                                                     start=(f == 0), stop=(f == NF - 1)),
                      reads=[wdnb, hb], writes=[bankb])
            P.add("dve", lambda e, d=d, bank=bank: e.scalar_tensor_tensor(out=xs[j][:, d, :], in0=bank[:, 0:TT], scalar=0.5 / ALPHA, in1=xs[j][:, d, :],
                                                                          op0=ALU.mult, op1=ALU.add),
                  reads=[bankb, xsb[j]], writes=[xsb[j]])

    def stage_ln_store(t):
        j = t % NB
        t0 = t * TT
        layer_norm_fm(C, P, xs[j], xsb[j], TT, lng, lnb, xs[j], xsb[j], tmp, eps=LN_EPS / (ALPHA * ALPHA))
        if dst_kind == "fm":
            P.add("act", lambda e: e.dma_start(out=dst_ap.rearrange("(c p) t -> p c t", p=128)[:, :, t0:t0 + TT], in_=xs[j]),
                  reads=[xsb[j]], writes=[dst_buf], lane=tag + "stx%d" % j)
        else:
            oj = t % 2
            for jj in range(2):
                for c in range(8):
                    pb = (jj * 8 + c) % 2
                    bank = C.psum[6 + pb][:, 0:128]
                    bankb = C.psumb[6 + pb]
                    P.add("pe", lambda e, jj=jj, c=c, bank=bank: e.transpose(bank[:, 0:128], xs[j][:, c, jj * 128:(jj + 1) * 128], C.ident_f32),
                          reads=[xsb[j], C.constb], writes=[bankb])
                    P.add("dve", lambda e, jj=jj, c=c, bank=bank: e.tensor_copy(out=otok[oj][:, jj, c * 128:(c + 1) * 128], in_=bank[:, 0:128]),
                          reads=[bankb], writes=[otokb[oj]])
            P.add("act", lambda e: e.dma_start(out=dst_ap[t0:t0 + TT, :].rearrange("(j p) d -> p j d", p=128), in_=otok[oj]),
                  reads=[otokb[oj]], writes=[dst_buf], lane=tag + "sto%d" % oj)

    stage_load(0)
    for t in range(ntiles):
        if t + 1 < ntiles:
            stage_load(t + 1)
        stage_cast(t)
        stage_ffn(t)
        stage_ln_store(t)
    P.barrier()
    ar.pop()


def evac_copy(P, i, out, in_, reads, writes):
    if i % 2 == 0:
        P.add("dve", lambda e: e.tensor_copy(out=out, in_=in_), reads=reads, writes=writes)
    else:
        P.add("act", lambda e: e.activation(out=out, in_=in_, func=AF.Copy), reads=reads, writes=writes)


def pass_proj(C, P, ar, D, w_in_ap, rope_ap, nseq, T, tag):
    nc = C.nc
    ar.push()
    NTOK = nseq * T
    NCH = N_IN // 128
    w = ar.bf16(8 * N_IN).rearrange("p (k n) -> p k n", k=8)
    wb = Buf(tag + "w", multi=True)
    wr = ar.bf16(8 * 1024).rearrange("p (k n) -> p k n", k=8)
    wrb = Buf(tag + "wr", multi=True)
    for k in range(8):
        P.add("pool", lambda e, k=k: e.dma_start(out=w[:, k, :], in_=w_in_ap[k * 128:(k + 1) * 128, :]),
              writes=[wb], lane="wgu%d" % (k % 4))
    QB = RW_COLS
    for k in range(8):
        for half in range(2):
            src = w[:, k, QB:QB + 1024].rearrange("p (g hf i) -> p g hf i", hf=2, i=32)[:, :, 1 - half, :]
            dst = wr[:, k, :].rearrange("p (g hf i) -> p g hf i", hf=2, i=32)[:, :, half, :]
            P.add("dve", lambda e, src=src, dst=dst: e.tensor_copy(out=dst, in_=src), reads=[wb], writes=[wrb])
    ntiles = NTOK // TT
    xs = [ar.f32(8 * TT).rearrange("p (c t) -> p c t", c=8) for _ in range(2)]
    xsb = [Buf(tag + "x%d" % j) for j in range(2)]
    xb = ar.bf16(8 * TT).rearrange("p (c t) -> p c t", c=8)
    xbb = Buf(tag + "xb")
    rw = [ar.f32(15 * TT).rearrange("p (c t) -> p c t", c=15) for _ in range(2)]
    rwb = [Buf(tag + "rw%d" % j) for j in range(2)]
    rp = [ar.f32(2 * TT).rearrange("p (c t) -> p c t", c=2) for _ in range(2)]
    rpb = [Buf(tag + "rp%d" % j) for j in range(2)]
    qk = [ar.bf16(8 * TT).rearrange("p (c t) -> p c t", c=8) for _ in range(2)]
    qkb = [Buf(tag + "qk%d" % j) for j in range(2)]
    t1 = [ar.f32(TT) for _ in range(2)]
    t1b = [Buf(tag + "t1%d" % j) for j in range(2)]
    t2 = [ar.f32(TT) for _ in range(2)]
    t2b = [Buf(tag + "t2%d" % j) for j in range(2)]
    vt = [ar.bf16(2 * 512).rearrange("p (j d) -> p j d", j=2) for _ in range(2)]
    vtb = [Buf(tag + "vt%d" % j) for j in range(2)]

    def stage_load(t):
        j = t % 2
        t0 = t * TT
        tp = t0 % T
        P.add("sp", lambda e: e.dma_start(out=xs[j], in_=D.X1[1].rearrange("(c p) t -> p c t", p=128)[:, :, t0:t0 + TT]),
              reads=[D.X1[0]], writes=[xsb[j]], lane=tag + "ldx%d" % j)
        P.add("sp", lambda e: e.dma_start(out=rp[j], in_=rope_ap[:, :, tp:tp + TT].rearrange("c p t -> p c t")),
              writes=[rpb[j]], lane=tag + "ldr%d" % j)

    def stage_main(t):
        j = t % 2
        t0 = t * TT
        P.add("act", lambda e: e.activation(out=xb, in_=xs[j], func=AF.Copy), reads=[xsb[j]], writes=[xbb])
        for c in range(15):
            bank = C.psum[c % 4]
            bankb = C.psumb[c % 4]
            for k in range(8):
                P.add("pe", lambda e, k=k, c=c, bank=bank: e.matmul(bank[:, 0:TT], lhsT=w[:, k, c * 128:(c + 1) * 128], rhs=xb[:, k, :],
                                                                     start=(k == 0), stop=(k == 7)), reads=[wb, xbb], writes=[bankb])
            evac_copy(P, c, rw[j][:, c, :], bank[:, 0:TT], [bankb], [rwb[j]])
        P.add("act", lambda e: e.dma_start(out=D.PRW[1].rearrange("(c p) t -> p c t", p=128)[:, :, t0:t0 + TT], in_=rw[j]),
              reads=[rwb[j]], writes=[D.PRW[0]], lane=tag + "strw%d" % j)
        for c in range(8):
            bank = C.psum[4 + c % 2]
            bankb = C.psumb[4 + c % 2]
            col0 = RW_COLS + c * 128
            for k in range(8):
                P.add("pe", lambda e, k=k, col0=col0, bank=bank: e.matmul(bank[:, 0:TT], lhsT=w[:, k, col0:col0 + 128], rhs=xb[:, k, :],
                                                                           start=(k == 0), stop=(k == 7)), reads=[wb, xbb], writes=[bankb])
            for k in range(8):
                P.add("pe", lambda e, k=k, c=c, bank=bank: e.matmul(bank[:, 256:256 + TT], lhsT=wr[:, k, c * 128:(c + 1) * 128], rhs=xb[:, k, :],
                                                                     start=(k == 0), stop=(k == 7)), reads=[wrb, xbb], writes=[bankb])
            a, ab = t1[c % 2], t1b[c % 2]
            b, bb = t2[c % 2], t2b[c % 2]
            P.add("dve", lambda e, bank=bank, a=a: e.tensor_tensor(out=a, in0=bank[:, 0:TT], in1=rp[j][:, 0, :], op=ALU.mult),
                  reads=[bankb, rpb[j]], writes=[ab])
            P.add("dve", lambda e, bank=bank, b=b: e.tensor_tensor(out=b, in0=bank[:, 256:256 + TT], in1=rp[j][:, 1, :], op=ALU.mult),
                  reads=[bankb, rpb[j]], writes=[bb])
            P.add("dve", lambda e, a=a, b=b, c=c: e.tensor_tensor(out=qk[j][:, c, :], in0=a, in1=b, op=ALU.add),
                  reads=[ab, bb], writes=[qkb[j]])
        P.add("act", lambda e: e.dma_start(out=D.QT[1].rearrange("(c p) t -> p c t", p=128)[:, :, t0:t0 + TT], in_=qk[j][:, 0:4, :]),
              reads=[qkb[j]], writes=[D.QT[0]], lane=tag + "stq%d" % j)
        P.add("act", lambda e: e.dma_start(out=D.KT[1].rearrange("(c p) t -> p c t", p=128)[:, :, t0:t0 + TT], in_=qk[j][:, 4:8, :]),
              reads=[qkb[j]], writes=[D.KT[0]], lane=tag + "stk%d" % j)
        vcol = RW_COLS + 1024
        for jj in range(2):
            bank = C.psum[6 + jj]
            bankb = C.psumb[6 + jj]
            for k in range(8):
                P.add("pe", lambda e, k=k, jj=jj, bank=bank: e.matmul(bank[:, 0:512], lhsT=xb[:, k, jj * 128:(jj + 1) * 128], rhs=w[:, k, vcol:vcol + 512],
                                                                       start=(k == 0), stop=(k == 7)), reads=[wb, xbb], writes=[bankb])
            evac_copy(P, jj, vt[j][:, jj, :], bank[:, 0:512], [bankb], [vtb[j]])
        P.add("act", lambda e: e.dma_start(out=D.VT[1][t0:t0 + TT, :].rearrange("(j p) d -> p j d", p=128), in_=vt[j]),
              reads=[vtb[j]], writes=[D.VT[0]], lane=tag + "stv%d" % j)

    stage_load(0)
    for t in range(ntiles):
        if t + 1 < ntiles:
            stage_load(t + 1)
        stage_main(t)
    P.barrier()
    ar.pop()


def pass_attn(C, P, ar, D, da_lambda_ap, subln_ap, l, lam_init, nseq, T, tag, lamc_ap=None):
    nc = C.nc
    ar.push()
    NKC = T // 128
    QT_ = 512
    nqt = T // QT_
    lam_t = ar.f32(256)
    lamb = Buf(tag + "lam")
    P.add("sp", lambda e: e.dma_start(out=lam_t, in_=da_lambda_ap[l].rearrange("a b -> (a b)").partition_broadcast(128)),
          writes=[lamb], lane=tag + "lam")
    sg = ar.f32(128)
    sgb = Buf(tag + "subg")
    P.add("sp", lambda e: e.dma_start(out=sg, in_=subln_ap[l].partition_broadcast(128)), writes=[sgb], lane=tag + "subg")
    lsc = ar.f32(8)
    lamc = ar.f32(2)
    lamcb = Buf(tag + "lamc")
    if lamc_ap is not None:
        P.add("sp", lambda e: e.dma_start(out=lamc, in_=lamc_ap[:, :]), writes=[lamcb], lane=tag + "lamc")
    else:
        P.add("dve", lambda e: e.memset(lamc[:, 0:1], -lam_init), writes=[lamcb])
        P.add("dve", lambda e: e.memset(lamc[:, 1:2], 1.0 - lam_init), writes=[lamcb])
    P.add("dve", lambda e: e.tensor_tensor(out=lam_t[:, 0:64], in0=lam_t[:, 0:64], in1=lam_t[:, 64:128], op=ALU.mult), reads=[lamb], writes=[lamb], fs=True)
    P.add("dve", lambda e: e.tensor_tensor(out=lam_t[:, 128:192], in0=lam_t[:, 128:192], in1=lam_t[:, 192:256], op=ALU.mult), reads=[lamb], writes=[lamb], fs=True)
    P.add("dve", lambda e: e.tensor_reduce(out=lsc[:, 0:1], in_=lam_t[:, 0:64], axis=AX.X, op=ALU.add), reads=[lamb], writes=[lamb], fs=True)
    P.add("dve", lambda e: e.tensor_reduce(out=lsc[:, 1:2], in_=lam_t[:, 128:192], axis=AX.X, op=ALU.add), reads=[lamb], writes=[lamb], fs=True)
    P.add("act", lambda e: e.activation(out=lsc[:, 2:4], in_=lsc[:, 0:2], func=AF.Exp), reads=[lamb], writes=[lamb], fs=True)
    P.add("dve", lambda e: e.tensor_tensor(out=lsc[:, 4:5], in0=lsc[:, 3:4], in1=lsc[:, 2:3], op=ALU.subtract), reads=[lamb], writes=[lamb], fs=True)
    P.add("dve", lambda e: e.tensor_scalar(out=lsc[:, 5:6], in0=lsc[:, 4:5], scalar1=lamc[:, 0:1], scalar2=None, op0=ALU.add), reads=[lamb, lamcb], writes=[lamb], fs=True)
    neg_lam = lsc[:, 5:6]
    P.add("dve", lambda e: e.tensor_scalar(out=sg, in0=sg, scalar1=lamc[:, 1:2], scalar2=None, op0=ALU.mult), reads=[sgb, lamcb], writes=[sgb], fs=True)

    kT = [ar.bf16(T) for _ in range(2)]
    kTb = [Buf(tag + "kT%d" % j) for j in range(2)]
    va = [ar.bf16(NKC * 130).rearrange("p (k v) -> p k v", v=130) for _ in range(2)]
    vab = [Buf(tag + "va%d" % j) for j in range(2)]
    for j in range(2):
        P.add("dve", lambda e, j=j: e.memset(va[j][:, :, 128:130], 1.0), writes=[vab[j]])
    qT = [ar.bf16(QT_) for _ in range(2)]
    qTb = [Buf(tag + "qT%d" % j) for j in range(2)]
    NPT = 3
    pT = [ar.bf16(QT_) for _ in range(NPT)]
    pTb = [Buf(tag + "pT%d" % j) for j in range(NPT)]
    o = ar.f32(4 * 128).rearrange("p (q v) -> p q v", q=4)
    ob = Buf(tag + "o")
    t1 = ar.f32(4 * 128).rearrange("p (q v) -> p q v", q=4)
    t1b = Buf(tag + "t1")
    rec = ar.f32(8)
    recb = Buf(tag + "rec")
    ss = ar.f32(8)
    ssb = Buf(tag + "ss")
    yb16 = ar.bf16(4 * 128).rearrange("p (q v) -> p q v", q=4)
    ybb = Buf(tag + "y")
    yT = [ar.bf16(QT_) for _ in range(2)]
    yTb = [Buf(tag + "yT%d" % j) for j in range(2)]
    SB = (0, 1, 2)
    AB = ((3, 4), (5, 6))
    TB = 7
    tbank = C.psum[TB].bitcast(BF16)
    scale = 1.0 / math.sqrt(64.0)
    sidx = 0
    it = 0
    for s_ in range(nseq):
        for h in range(4):
            j = it % 2
            it += 1
            P.add("sp", lambda e, j=j, h=h, s_=s_: e.dma_start(out=kT[j], in_=D.KT[1][h * 128:(h + 1) * 128, s_ * T:(s_ + 1) * T]),
                  reads=[D.KT[0]], writes=[kTb[j]], lane=tag + "ldk%d" % j)
            P.add("sp", lambda e, j=j, h=h, s_=s_: e.dma_start(
                out=va[j][:, :, 0:128], in_=D.VT[1][s_ * T:(s_ + 1) * T, h * 128:(h + 1) * 128].rearrange("(k p) v -> p k v", p=128)),
                reads=[D.VT[0]], writes=[vab[j]], lane=tag + "ldv%d" % j)
            for qt in range(nqt):
                jq = qt % 2
                q0 = s_ * T + qt * QT_
                P.add("sp", lambda e, jq=jq, h=h, q0=q0: e.dma_start(out=qT[jq], in_=D.QT[1][h * 128:(h + 1) * 128, q0:q0 + QT_]),
                      reads=[D.QT[0]], writes=[qTb[jq]], lane=tag + "ldq%d" % jq)
                for c in range(2):
                    banks = AB[c]

                    def acc(qb):
                        bk = banks[0] if qb < 3 else banks[1]
                        col = (qb % 3) * 130
                        return C.psum[bk][:, col:col + 129], C.psumb[bk]
                    for kc in range(NKC):
                        sb = SB[sidx % 3]
                        pj = sidx % NPT
                        sidx += 1
                        P.add("pe", lambda e, sb=sb, kc=kc, c=c, j=j, jq=jq: e.matmul(
                            C.psum[sb][:, 0:QT_], lhsT=kT[j][c * 64:(c + 1) * 64, kc * 128:(kc + 1) * 128], rhs=qT[jq][c * 64:(c + 1) * 64, :],
                            start=True, stop=True), reads=[kTb[j], qTb[jq]], writes=[C.psumb[sb]])
                        P.add("act", lambda e, sb=sb, pj=pj: e.activation(out=pT[pj], in_=C.psum[sb][:, 0:QT_], func=AF.Exp, scale=scale),
                              reads=[C.psumb[sb]], writes=[pTb[pj]])
                        for qb in range(4):
                            a_ap, a_b = acc(qb)
                            P.add("pe", lambda e, a_ap=a_ap, pj=pj, qb=qb, kc=kc, j=j: e.matmul(
                                a_ap, lhsT=pT[pj][:, qb * 128:(qb + 1) * 128], rhs=va[j][:, kc, 0:129],
                                start=(kc == 0 and qb in (0, 3)), stop=(kc == NKC - 1 and qb in (2, 3)), skip_group_check=True),
                                reads=[pTb[pj], vab[j]], writes=[a_b])
                    for qb in range(4):
                        a_ap, a_b = acc(qb)
                        P.add("dve", lambda e, a_ap=a_ap, qb=qb: e.reciprocal(out=rec[:, qb:qb + 1], in_=a_ap[:, 128:129]), reads=[a_b], writes=[recb])
                    for qb in range(4):
                        a_ap, a_b = acc(qb)
                        if c == 0:
                            P.add("dve", lambda e, a_ap=a_ap, qb=qb: e.tensor_scalar(out=o[:, qb, :], in0=a_ap[:, 0:128], scalar1=rec[:, qb:qb + 1],
                                                                                     scalar2=None, op0=ALU.mult), reads=[a_b, recb], writes=[ob], fs=True)
                        else:
                            P.add("dve", lambda e, a_ap=a_ap, qb=qb: e.tensor_scalar(out=t1[:, qb, :], in0=a_ap[:, 0:128], scalar1=rec[:, qb:qb + 1],
                                                                                     scalar2=None, op0=ALU.mult), reads=[a_b, recb], writes=[t1b], fs=True)
                    if c == 1:
                        P.add("dve", lambda e: e.scalar_tensor_tensor(out=o, in0=t1, scalar=neg_lam, in1=o,
                                                                      op0=ALU.mult, op1=ALU.add), reads=[t1b, ob, lamb], writes=[ob])
                P.add("dve", lambda e: e.tensor_tensor(out=t1, in0=o, in1=o, op=ALU.mult), reads=[ob], writes=[t1b])
                P.add("dve", lambda e: e.tensor_reduce(out=ss[:, 0:4], in_=t1, axis=AX.X, op=ALU.add), reads=[t1b], writes=[ssb])
                P.add("act", lambda e: e.activation(out=ss[:, 0:4], in_=ss[:, 0:4], func=AF.Sqrt, bias=SUBLN_EPS, scale=1.0 / 128.0), reads=[ssb], writes=[ssb])
                P.add("dve", lambda e: e.reciprocal(out=ss[:, 4:8], in_=ss[:, 0:4]), reads=[ssb], writes=[ssb])
                P.add("dve", lambda e: e.tensor_tensor(out=o, in0=o, in1=ss[:, 4:8].unsqueeze(2).to_broadcast([128, 4, 128]), op=ALU.mult),
                      reads=[ob, ssb], writes=[ob], fs=True)
                P.add("dve", lambda e: e.tensor_tensor(out=yb16, in0=o, in1=sg.unsqueeze(1).to_broadcast([128, 4, 128]), op=ALU.mult),
                      reads=[ob, sgb], writes=[ybb])
                jy = qt % 2
                for qb in range(4):
                    P.add("pe", lambda e, qb=qb: e.transpose(tbank[:, qb * 128:(qb + 1) * 128], yb16[:, qb, :], C.ident_bf), reads=[ybb, C.constb],
                          writes=[C.psumb[TB]])
                P.add("dve", lambda e, jy=jy: e.tensor_copy(out=yT[jy], in_=tbank[:, 0:QT_]), reads=[C.psumb[TB]], writes=[yTb[jy]])
                P.add("act", lambda e, jy=jy, h=h, q0=q0: e.dma_start(out=D.YDA[1][h * 128:(h + 1) * 128, q0:q0 + QT_], in_=yT[jy]),
                      reads=[yTb[jy]], writes=[D.YDA[0]], lane=tag + "sty%d" % jy)
    P.barrier()
    ar.pop()


NEG_EH = -math.exp(-0.5)
CH = 64


def load_cols(C, P, ar, src_1d, nchunk, name, eng="sp"):
    t = ar.f32(nchunk)
    b = Buf(name)
    P.add(eng, lambda e: e.dma_start(out=t, in_=src_1d.rearrange("(c p) -> p c", p=128), allow_slow_non_contiguous=True),
          writes=[b], lane=name)
    return t, b


def pass_rwkv_prep(C, P, ar, D, W, l, nseq, T, tag):
    nc = C.nc
    ar.push()
    NTOK = nseq * T
    mup, mupb = load_cols(C, P, ar, W["shift_mu"][l, 0], 15, tag + "mup")
    mun, munb = load_cols(C, P, ar, W["shift_mu"][l, 1], 15, tag + "mun")
    coef = ar.f32(15)
    P.add("dve", lambda e: e.tensor_tensor(out=coef, in0=mup, in1=mun, op=ALU.add), reads=[mupb, munb], writes=[mupb])
    P.add("dve", lambda e: e.tensor_scalar(out=coef, in0=coef, scalar1=-1.0, scalar2=1.0, op0=ALU.mult, op1=ALU.add),
          reads=[mupb], writes=[mupb], fs=True)
    wbias = [load_cols(C, P, ar, W["rw_w_bias"][l, d], 4, tag + "wb%d" % d) for d in range(2)]
    abias = [load_cols(C, P, ar, W["rw_a_bias"][l, d], 4, tag + "ab%d" % d) for d in range(2)]
    kk_s, kk_sb = load_cols(C, P, ar, W["rw_k_k"][l], 4, tag + "kk")
    ka_s, ka_sb = load_cols(C, P, ar, W["rw_k_a"][l], 4, tag + "ka")
    rk_s, rk_sb = load_cols(C, P, ar, W["rw_r_k"][l].rearrange("h n -> (h n)"), 4, tag + "rk")
    omka = ar.f32(4)
    P.add("dve", lambda e: e.tensor_scalar(out=omka, in0=ka_s, scalar1=-1.0, scalar2=1.0, op0=ALU.mult, op1=ALU.add),
          reads=[ka_sb], writes=[ka_sb])
    prm = [mupb, munb, wbias[0][1], wbias[1][1], abias[0][1], abias[1][1], kk_sb, ka_sb, rk_sb]
    wup = ar.bf16(512)
    aup = ar.bf16(512)
    gup = ar.bf16(512)
    lwb = Buf(tag + "lora", multi=True)
    P.add("pool", lambda e: e.dma_start(out=wup, in_=W["rw_w_up"][l].rearrange("d r c -> (d r) c")), writes=[lwb], lane="wgu0")
    P.add("pool", lambda e: e.dma_start(out=aup, in_=W["rw_a_up"][l].rearrange("d r c -> (d r) c")), writes=[lwb], lane="wgu1")
    P.add("pool", lambda e: e.dma_start(out=gup, in_=W["rw_g_up"][l]), writes=[lwb], lane="wgu2")
    bones = ar.bf16(128)
    bonesb = Buf(tag + "bones")
    P.add("dve", lambda e: e.memset(bones, 0.0), writes=[bonesb])
    P.add("dve", lambda e: e.memset(bones[0:64, 0:64], 1.0), writes=[bonesb])
    P.add("dve", lambda e: e.memset(bones[64:128, 64:128], 1.0), writes=[bonesb])
    smask = ar.f32(TT)
    smb = Buf(tag + "smask")
    P.add("dve", lambda e: e.memset(smask, 1.0), writes=[smb])
    P.add("dve", lambda e: e.memset(smask.rearrange("p (c t) -> p c t", t=CH)[:, :, 0:1], 0.0), writes=[smb])

    def f3(n=4):
        return ar.f32(n * TT).rearrange("p (c t) -> p c t", c=n)

    def b3(n=4):
        return ar.bf16(n * TT).rearrange("p (c t) -> p c t", c=n)
    pr = ar.f32(15 * (TT + 2)).rearrange("p (c t) -> p c t", c=15)
    prb = Buf(tag + "pr")
    sh = f3(15)
    shb = Buf(tag + "sh")
    twd = ar.bf16(TT); adb16 = ar.bf16(TT); sgd = ar.bf16(TT)
    smallb = Buf(tag + "small")
    sw = [f3(), f3()]; swb = [Buf(tag + "sw0"), Buf(tag + "sw1")]
    aa = [f3(), f3()]; aab = [Buf(tag + "a0"), Buf(tag + "a1")]
    tkk = f3(); tkkb = Buf(tag + "tkk")
    sq = b3(); sqb = Buf(tag + "sq")
    rn = f3(); rnb = Buf(tag + "rn")
    kk = f3(); kkb = Buf(tag + "kkn")
    km = [f3(), f3()]; kmb = [Buf(tag + "km0"), Buf(tag + "km1")]
    tmp = f3(); tmpb = Buf(tag + "tmp")
    Lc = f3(); Lcb = Buf(tag + "L")
    Ep = f3(); Epb = Buf(tag + "Ep")
    Em = f3(); Emb = Buf(tag + "Em")
    Eq = f3(); Eqb = Buf(tag + "Eq")
    G = f3(); Gb = Buf(tag + "G")
    bon = f3(); bonb = Buf(tag + "bon")
    gc = ar.f32(16).rearrange("p (c k) -> p c k", c=4); gcb = Buf(tag + "gc")
    outs = [b3() for _ in range(8)]
    outb = [Buf(tag + "out%d" % i) for i in range(8)]
    vb16 = b3(); vb16b = Buf(tag + "vb16")
    vtok = ar.bf16(2 * 512).rearrange("p (j d) -> p j d", j=2); vtokb = Buf(tag + "vtok")
    tbank = C.psum[7].bitcast(BF16)
    PS = C.psum
    PB = C.psumb
    ntiles = NTOK // TT
    for t in range(ntiles):
        t0 = t * TT
        first = (t0 % T == 0)
        last = ((t0 + TT) % T == 0)
        lo = 0 if not first else 1
        hi = TT + 2 if not last else TT + 1
        src = D.PRW[1].rearrange("(c p) t -> p c t", p=128)[:, :, t0 - 1 + lo:t0 - 1 + hi]
        P.add("sp", lambda e, src=src, lo=lo, hi=hi: e.dma_start(out=pr[:, :, lo:hi], in_=src), reads=[D.PRW[0]], writes=[prb], lane=tag + "ldpr")
        if first:
            P.add("dve", lambda e: e.memset(pr[:, :, 0:1], 0.0), writes=[prb])
        if last:
            P.add("dve", lambda e: e.memset(pr[:, :, TT + 1:TT + 2], 0.0), writes=[prb])
        for c in range(15):
            P.add("act", lambda e, c=c: e.activation(out=sh[:, c, :], in_=pr[:, c, 1:TT + 1], func=AF.Copy, scale=coef[:, c:c + 1]),
                  reads=[prb, mupb], writes=[shb])
            P.add("dve", lambda e, c=c: e.scalar_tensor_tensor(out=sh[:, c, :], in0=pr[:, c, 0:TT], scalar=mup[:, c:c + 1], in1=sh[:, c, :],
                                                               op0=ALU.mult, op1=ALU.add), reads=[prb, mupb, shb], writes=[shb])
            P.add("dve", lambda e, c=c: e.scalar_tensor_tensor(out=sh[:, c, :], in0=pr[:, c, 2:TT + 2], scalar=mun[:, c:c + 1], in1=sh[:, c, :],
                                                               op0=ALU.mult, op1=ALU.add), reads=[prb, munb, shb], writes=[shb])
        r_ = sh[:, 0:4, :]; k_ = sh[:, 4:8, :]; v_ = sh[:, 8:12, :]
        P.add("act", lambda e: e.activation(out=twd, in_=sh[:, 12, :], func=AF.Tanh), reads=[shb], writes=[smallb])
        P.add("act", lambda e: e.activation(out=adb16, in_=sh[:, 13, :], func=AF.Copy), reads=[shb], writes=[smallb])
        P.add("act", lambda e: e.activation(out=sgd, in_=sh[:, 14, :], func=AF.Sigmoid), reads=[shb], writes=[smallb])
        bi = 0
        for d in range(2):
            for c in range(4):
                bk = bi % 4; bi += 1
                P.add("pe", lambda e, d=d, c=c, bk=bk: e.matmul(PS[bk][:, 0:TT], lhsT=wup[d * 64:(d + 1) * 64, c * 128:(c + 1) * 128],
                                                                 rhs=twd[d * 64:(d + 1) * 64, :], start=True, stop=True),
                      reads=[lwb, smallb], writes=[PB[bk]])
                P.add("act", lambda e, d=d, c=c, bk=bk: e.activation(out=sw[d][:, c, :], in_=PS[bk][:, 0:TT], func=AF.Sigmoid,
                                                                     bias=wbias[d][0][:, c:c + 1], scale=1.0),
                      reads=[PB[bk], wbias[d][1]], writes=[swb[d]])
        for d in range(2):
            for c in range(4):
                bk = bi % 4; bi += 1
                P.add("pe", lambda e, d=d, c=c, bk=bk: e.matmul(PS[bk][:, 0:TT], lhsT=aup[d * 64:(d + 1) * 64, c * 128:(c + 1) * 128],
                                                                 rhs=adb16[d * 64:(d + 1) * 64, :], start=True, stop=True),
                      reads=[lwb, smallb], writes=[PB[bk]])
                P.add("act", lambda e, d=d, c=c, bk=bk: e.activation(out=aa[d][:, c, :], in_=PS[bk][:, 0:TT], func=AF.Sigmoid,
                                                                     bias=abias[d][0][:, c:c + 1], scale=1.0),
                      reads=[PB[bk], abias[d][1]], writes=[aab[d]])
        for c in range(4):
            bk = bi % 4; bi += 1
            P.add("pe", lambda e, c=c, bk=bk: e.matmul(PS[bk][:, 0:TT], lhsT=gup[:, c * 128:(c + 1) * 128], rhs=sgd, start=True, stop=True),
                  reads=[lwb, smallb], writes=[PB[bk]])
            P.add("dve", lambda e, c=c, bk=bk: e.tensor_copy(out=G[:, c, :], in_=PS[bk][:, 0:TT]), reads=[PB[bk]], writes=[Gb])
        P.add("act", lambda e, t0=t0: e.dma_start(out=D.GT[1].rearrange("(c p) t -> p c t", p=128)[:, :, t0:t0 + TT], in_=G),
              reads=[Gb], writes=[D.GT[0]], lane=tag + "stg")
        for c in range(4):
            P.add("act", lambda e, c=c: e.activation(out=tkk[:, c, :], in_=sh[:, 4 + c, :], func=AF.Copy, scale=kk_s[:, c:c + 1]),
                  reads=[shb, kk_sb], writes=[tkkb])
        P.add("dve", lambda e: e.tensor_tensor(out=sq, in0=tkk, in1=tkk, op=ALU.mult), reads=[tkkb], writes=[sqb])
        for c in range(4):
            bk = 4 + c // 2
            P.add("pe", lambda e, c=c, bk=bk: e.matmul(PS[bk][:, (c % 2) * TT:(c % 2 + 1) * TT], lhsT=bones, rhs=sq[:, c, :], start=True, stop=True),
                  reads=[bonesb, sqb], writes=[PB[bk]])
        for hb in range(2):
            bk = 4 + hb
            rnv = rn[:, 2 * hb:2 * hb + 2, :]
            P.add("dve", lambda e, bk=bk, rnv=rnv: e.tensor_scalar(out=rnv, in0=PS[bk][:, 0:2 * TT].rearrange("p (c t) -> p c t", c=2),
                                                                    scalar1=1e-24, scalar2=None, op0=ALU.max), reads=[PB[bk]], writes=[rnb])
        P.add("act", lambda e: e.activation(out=rn, in_=rn, func=AF.Sqrt), reads=[rnb], writes=[rnb])
        P.add("dve", lambda e: e.reciprocal(out=rn, in_=rn), reads=[rnb], writes=[rnb])
        P.add("dve", lambda e: e.tensor_tensor(out=kk, in0=tkk, in1=rn, op=ALU.mult), reads=[tkkb, rnb], writes=[kkb])
        for d in range(2):
            for c in range(4):
                P.add("dve", lambda e, d=d, c=c: e.tensor_scalar(out=km[d][:, c, :], in0=aa[d][:, c, :], scalar1=ka_s[:, c:c + 1],
                                                                 scalar2=omka[:, c:c + 1], op0=ALU.mult, op1=ALU.add),
                      reads=[aab[d], ka_sb], writes=[kmb[d]])
            P.add("dve", lambda e, d=d: e.tensor_tensor(out=km[d], in0=km[d], in1=k_, op=ALU.mult), reads=[kmb[d], shb], writes=[kmb[d]])
        P.add("dve", lambda e: e.tensor_tensor(out=tmp, in0=km[0], in1=km[1], op=ALU.add), reads=[kmb[0], kmb[1]], writes=[tmpb])
        P.add("dve", lambda e: e.tensor_tensor(out=tmp, in0=tmp, in1=r_, op=ALU.mult), reads=[tmpb, shb], writes=[tmpb])
        for c in range(4):
            P.add("act", lambda e, c=c: e.activation(out=sq[:, c, :], in_=tmp[:, c, :], func=AF.Copy, scale=rk_s[:, c:c + 1]),
                  reads=[tmpb, rk_sb], writes=[sqb])
        for c in range(4):
            bk = 4 + c // 2
            P.add("pe", lambda e, c=c, bk=bk: e.matmul(PS[bk][:, (c % 2) * TT:(c % 2 + 1) * TT], lhsT=bones, rhs=sq[:, c, :], start=True, stop=True),
                  reads=[bonesb, sqb], writes=[PB[bk]])
        for hb in range(2):
            bk = 4 + hb
            P.add("dve", lambda e, bk=bk, hb=hb: e.tensor_tensor(out=bon[:, 2 * hb:2 * hb + 2, :], in0=PS[bk][:, 0:2 * TT].rearrange("p (c t) -> p c t", c=2),
                                                                  in1=sh[:, 8 + 2 * hb:10 + 2 * hb, :], op=ALU.mult), reads=[PB[bk], shb], writes=[bonb])
        P.add("act", lambda e, t0=t0: e.dma_start(out=D.BON[1].rearrange("(c p) t -> p c t", p=128)[:, :, t0:t0 + TT], in_=bon),
              reads=[bonb], writes=[D.BON[0]], lane=tag + "stbon")
        P.add("act", lambda e: e.activation(out=vb16, in_=v_, func=AF.Copy), reads=[shb], writes=[vb16b])
        for jj in range(2):
            for c in range(4):
                P.add("pe", lambda e, jj=jj, c=c: e.transpose(tbank[:, (jj * 4 + c) * 128:(jj * 4 + c + 1) * 128], vb16[:, c, jj * 128:(jj + 1) * 128], C.ident_bf),
                      reads=[vb16b, C.constb], writes=[PB[7]])
        P.add("dve", lambda e: e.tensor_copy(out=vtok, in_=tbank[:, 0:1024].rearrange("p (j d) -> p j d", j=2)), reads=[PB[7]], writes=[vtokb])
        P.add("act", lambda e, t0=t0: e.dma_start(out=D.VTOK[1][t0:t0 + TT, :].rearrange("(j p) d -> p j d", p=128), in_=vtok),
              reads=[vtokb], writes=[D.VTOK[0]], lane=tag + "stvt")
        for d in range(2):
            for c in range(4):
                P.add("dve", lambda e, d=d, c=c: e.tensor_tensor_scan(out=Lc[:, c, :], data0=smask, data1=sw[d][:, c, :], initial=0.0,
                                                                      op0=ALU.mult, op1=ALU.add), reads=[smb, swb[d]], writes=[Lcb])
            P.add("act", lambda e: e.activation(out=gc, in_=Lc.rearrange("p c (k t) -> p c k t", t=CH)[:, :, :, CH - 1], func=AF.Exp, scale=NEG_EH),
                  reads=[Lcb], writes=[gcb])
            P.add("act", lambda e, t0=t0, d=d: e.dma_start(out=D.GC[1][d].rearrange("(c p) k -> p c k", p=128)[:, :, t0 // CH:t0 // CH + TT // CH], in_=gc),
                  reads=[gcb], writes=[D.GC[0]], lane=tag + "stgc")
            if d == 1:
                Lv = Lc.rearrange("p c (k t) -> p c k t", t=CH)
                totb = Lv[:, :, :, CH - 1:CH].to_broadcast([128, 4, TT // CH, CH])
                P.add("dve", lambda e: e.tensor_tensor(out=tmp, in0=sw[1], in1=Lc, op=ALU.subtract), reads=[swb[1], Lcb], writes=[tmpb])
                P.add("dve", lambda e, totb=totb: e.tensor_tensor(out=tmp.rearrange("p c (k t) -> p c k t", t=CH), in0=tmp.rearrange("p c (k t) -> p c k t", t=CH),
                                                                    in1=totb, op=ALU.add), reads=[tmpb, Lcb], writes=[tmpb])
                P.add("dve", lambda e: e.tensor_copy(out=Lc, in_=tmp), reads=[tmpb], writes=[Lcb])
            P.add("act", lambda e: e.activation(out=Ep, in_=Lc, func=AF.Exp, scale=NEG_EH), reads=[Lcb], writes=[Epb])
            P.add("act", lambda e: e.activation(out=Em, in_=Lc, func=AF.Exp, scale=-NEG_EH), reads=[Lcb], writes=[Emb])
            P.add("dve", lambda e, d=d: e.tensor_tensor(out=tmp, in0=Lc, in1=sw[d], op=ALU.subtract), reads=[Lcb, swb[d]], writes=[tmpb])
            P.add("act", lambda e: e.activation(out=Eq, in_=tmp, func=AF.Exp, scale=NEG_EH), reads=[tmpb], writes=[Eqb])
            o_aG, o_rG, o_bG, o_kG = (outs[d * 4 + i] for i in range(4))
            ob_ = [outb[d * 4 + i] for i in range(4)]
            P.add("dve", lambda e, o_aG=o_aG: e.scalar_tensor_tensor(out=o_aG, in0=kk, scalar=-1.0, in1=Eq, op0=ALU.mult, op1=ALU.mult),
                  reads=[kkb, Eqb], writes=[ob_[0]])
            P.add("dve", lambda e, o_rG=o_rG: e.tensor_tensor(out=o_rG, in0=r_, in1=Ep, op=ALU.mult), reads=[shb, Epb], writes=[ob_[1]])
            P.add("dve", lambda e, d=d: e.tensor_tensor(out=tmp, in0=kk, in1=aa[d], op=ALU.mult), reads=[kkb, aab[d]], writes=[tmpb])
            P.add("dve", lambda e, o_bG=o_bG: e.tensor_tensor(out=o_bG, in0=tmp, in1=Em, op=ALU.mult), reads=[tmpb, Emb], writes=[ob_[2]])
            P.add("dve", lambda e, o_kG=o_kG, d=d: e.tensor_tensor(out=o_kG, in0=km[d], in1=Em, op=ALU.mult), reads=[kmb[d], Emb], writes=[ob_[3]])
            for i in range(4):
                P.add("act", lambda e, d=d, i=i, t0=t0: e.dma_start(out=D.OPS[1][d * 4 + i].rearrange("(c p) t -> p c t", p=128)[:, :, t0:t0 + TT], in_=outs[d * 4 + i]),
                      reads=[ob_[i]], writes=[D.OPS[0]], lane=tag + "stop%d" % (d * 4 + i))
    P.barrier()
    ar.pop()


def rw_masks():
    tt = np.arange(64)
    out = np.zeros((2, 64, 192), dtype=np.float32)
    for d in range(2):
        if d == 0:
            ms = (tt[:, None] < tt[None, :]).astype(np.float32)
            mi = (tt[:, None] <= tt[None, :]).astype(np.float32)
        else:
            ms = (tt[:, None] > tt[None, :]).astype(np.float32)
            mi = (tt[:, None] >= tt[None, :]).astype(np.float32)
        out[d, :, 0:64] = ms
        out[d, :, 64:128] = mi
        out[d, :, 128:192] = ms.T
    return out


def pass_rwkv_scan(C, P, ar, D, mask_ap, nseq, T, tag):
    nc = C.nc
    ar.push()
    NCK = TT // CH
    ntile_seq = T // TT
    PS, PB = C.psum, C.psumb
    mk = ar.f32(2 * 192, parts=64).rearrange("p (d m) -> p d m", d=2)
    mkb = Buf(tag + "mask")
    P.add("sp", lambda e: e.dma_start(out=mk, in_=mask_ap.rearrange("d p m -> p d m")), writes=[mkb], lane=tag + "mask")

    def t64(n, dt="bf16"):
        return (ar.bf16 if dt == "bf16" else ar.f32)(n, parts=64)
    RH = [t64(8 * 2 * TT) for _ in range(2)]
    RHb = [Buf(tag + "RH%d" % i, multi=True) for i in range(2)]
    LH = [t64(8 * 2 * TT) for _ in range(2)]
    LHb = [Buf(tag + "LH%d" % i, multi=True) for i in range(2)]
    V0 = [t64(NCK * 512).rearrange("p (c h i) -> p c h i", c=NCK, h=8) for _ in range(2)]
    V0b = [Buf(tag + "V0%d" % i) for i in range(2)]
    GCt = [t64(8 * NCK, "f32").rearrange("p (h k) -> p h k", h=8) for _ in range(2)]
    GCb = [Buf(tag + "GC%d" % i) for i in range(2)]
    ost = [t64(8 * TT, "f32").rearrange("p (h t) -> p h t", h=8) for _ in range(2)]
    ostb = [Buf(tag + "ost%d" % i) for i in range(2)]
    ATb = [t64(8 * 128).rearrange("p (h t) -> p h t", h=8) for _ in range(2)]
    ATbb = [Buf(tag + "ATb%d" % i) for i in range(2)]
    ATk = [t64(8 * 128).rearrange("p (h t) -> p h t", h=8) for _ in range(2)]
    ATkb = [Buf(tag + "ATk%d" % i) for i in range(2)]
    Sf = [t64(512).rearrange("p (h t) -> p h t", h=8) for _ in range(2)]
    Sfb = [Buf(tag + "Sf%d" % i) for i in range(2)]
    BKT = [t64(1024).rearrange("p (k h j) -> p k h j", k=2, h=8) for _ in range(2)]
    BKTb = [Buf(tag + "BKT%d" % i) for i in range(2)]
    Wp = [t64(512).rearrange("p (h t) -> p h t", h=8) for _ in range(2)]
    Wpb = [Buf(tag + "W%d" % i) for i in range(2)]
    WTp = [t64(512).rearrange("p (h t) -> p h t", h=8) for _ in range(2)]
    WTpb = [Buf(tag + "WT%d" % i) for i in range(2)]
    Sp = [t64(512).rearrange("p (h t) -> p h t", h=8) for _ in range(2)]
    Spb = [Buf(tag + "S%d" % i) for i in range(2)]
    RS = t64(512).rearrange("p (h i) -> p h i", h=8)
    RSb = Buf(tag + "RS")
    U = t64(512).rearrange("p (h i) -> p h i", h=8)
    Ub = Buf(tag + "U")
    H = t64(512, "f32").rearrange("p (h i) -> p h i", h=8)
    Hb16 = t64(512).rearrange("p (h i) -> p h i", h=8)
    Ht = t64(512, "f32").rearrange("p (h i) -> p h i", h=8)
    Hbuf = Buf(tag + "H")
    Hbb = Buf(tag + "Hb")
    Htb = Buf(tag + "Ht")
    tb3 = PS[3].bitcast(BF16)
    identb = C.ident_bf[0:64, 0:64]
    ident_b3 = C.ident_bf[0:64, 0:64].unsqueeze(1).to_broadcast([64, 8, 64])
    tile_ctr = [0]
    chunk_ctr = [0]

    def v3(bank, n=8):
        return PS[bank][0:64, 0:512].rearrange("p (h t) -> p h t", h=n)

    def load_tile(d, s_, ti):
        j = tile_ctr[0] % 2
        tile_ctr[0] += 1
        t0 = s_ * T + ti * TT
        v5 = lambda x: x.rearrange("p (h c k t) -> p h c k t", h=8, c=NCK, k=2)
        for kind in range(4):
            dbuf = RH[j] if kind < 2 else LH[j]
            bb = RHb[j] if kind < 2 else LHb[j]
            for ck in range(NCK):
                dst = v5(dbuf)[:, :, ck, kind % 2, :]
                src = D.OPS[1][d * 4 + kind].rearrange("(h j) t -> j h t", j=64)[:, :, t0 + ck * CH:t0 + (ck + 1) * CH]
                P.add("sp", lambda e, src=src, dst=dst: e.dma_start(out=dst, in_=src), reads=[D.OPS[0]], writes=[bb], lane=tag + "ldop%d_%d" % (kind, j))
        src = D.VTOK[1][t0:t0 + TT, :].rearrange("(c s) f -> s c f", s=64)
        P.add("sp", lambda e, src=src, j=j: e.dma_start(out=V0[j].rearrange("s c h i -> s c (h i)"), in_=src),
              reads=[D.VTOK[0]], writes=[V0b[j]], lane=tag + "lduv%d" % j)
        src = D.GC[1][d].rearrange("(h j) k -> j h k", j=64)[:, :, t0 // CH:t0 // CH + NCK]
        P.add("sp", lambda e, src=src, j=j: e.dma_start(out=GCt[j], in_=src, allow_slow_non_contiguous=True), reads=[D.GC[0]], writes=[GCb[j]],
              lane=tag + "ldgc%d" % j)
        return j, t0

    def opnd(buf, h, ck, lo=0, hi=128):
        base = (h * NCK + ck) * 128
        return buf[:, base + lo:base + hi]

    def precompute(d, j, ck):
        q = chunk_ctr[0] % 2
        chunk_ctr[0] += 1
        mask2 = mk[:, d, 0:128].unsqueeze(1).to_broadcast([64, 4, 128])
        for which, (dst, dstb, lo) in enumerate(((ATb[q], ATbb[q], 0), (ATk[q], ATkb[q], 64))):
            for h in range(8):
                bk = 2 * which + h // 4
                P.add("pe", lambda e, h=h, bk=bk, lo=lo: e.matmul(PS[bk][0:64, (h % 4) * 128:(h % 4 + 1) * 128], lhsT=opnd(LH[j], h, ck, lo, lo + 64),
                                                                   rhs=opnd(RH[j], h, ck), start=True, stop=True), reads=[LHb[j], RHb[j]], writes=[PB[bk]])
            for g in range(2):
                bk = 2 * which + g
                P.add("dve", lambda e, bk=bk, g=g, dst=dst: e.tensor_tensor(out=dst[:, g * 4:g * 4 + 4, :], in0=PS[bk][0:64, 0:512].rearrange("p (h t) -> p h t", h=4),
                                                                            in1=mask2, op=ALU.mult), reads=[PB[bk], mkb], writes=[dstb])
        for h in range(8):
            P.add("pe", lambda e, h=h: e.matmul(PS[0][0:64, h * 64:(h + 1) * 64], lhsT=opnd(RH[j], h, ck, 0, 64), rhs=opnd(LH[j], h, ck, 0, 64), start=True, stop=True),
                  reads=[LHb[j], RHb[j]], writes=[PB[0]])
        P.add("dve", lambda e: e.tensor_tensor(out=WTp[0], in0=v3(0), in1=mk[:, d, 128:192].unsqueeze(1).to_broadcast([64, 8, 64]), op=ALU.mult),
              reads=[PB[0], mkb], writes=[WTpb[0]])
        for kd in range(2):
            for h in range(8):
                P.add("pe", lambda e, h=h, kd=kd: e.transpose(tb3[0:64, (kd * 8 + h) * 64:(kd * 8 + h + 1) * 64], opnd(LH[j], h, ck, kd * 64, kd * 64 + 64), identb),
                      reads=[LHb[j], C.constb], writes=[PB[3]])
        P.add("act", lambda e: e.activation(out=BKT[q], in_=tb3[0:64, 0:1024].rearrange("p (k h j) -> p k h j", k=2, h=8), func=AF.Copy),
              reads=[PB[3]], writes=[BKTb[q]])
        W0 = ATb[q][:, :, 0:64]
        P.add("dve", lambda e: e.tensor_tensor(out=Sp[0], in0=W0, in1=ident_b3, op=ALU.add), reads=[ATbb[q], C.constb], writes=[Spb[0]])
        Wc, Wcb = W0, ATbb[q]
        WTc, WTcb = WTp[0], WTpb[0]
        Sc, Scb = Sp[0], Spb[0]
        for lvl in range(5):
            Wn, Wnb = Wp[lvl % 2], Wpb[lvl % 2]
            WTn, WTnb = WTp[(lvl + 1) % 2], WTpb[(lvl + 1) % 2]
            last = (lvl == 4)
            Sn, Snb = (Sf[q], Sfb[q]) if last else (Sp[(lvl + 1) % 2], Spb[(lvl + 1) % 2])
            if not last:
                for h in range(8):
                    P.add("pe", lambda e, h=h, WTc=WTc, Wc=Wc: e.matmul(PS[0][0:64, h * 64:(h + 1) * 64], lhsT=WTc[:, h, :], rhs=Wc[:, h, :], start=True, stop=True),
                          reads=[WTcb, Wcb], writes=[PB[0]])
            for h in range(8):
                P.add("pe", lambda e, h=h, WTc=WTc, Wc=Wc: e.matmul(PS[1][0:64, h * 64:(h + 1) * 64], lhsT=Wc[:, h, :], rhs=WTc[:, h, :], start=True, stop=True),
                      reads=[WTcb, Wcb], writes=[PB[1]])
            if not last:
                P.add("act", lambda e, Wn=Wn: e.activation(out=Wn, in_=v3(0), func=AF.Copy), reads=[PB[0]], writes=[Wnb])
            P.add("dve", lambda e, WTn=WTn: e.tensor_copy(out=WTn, in_=v3(1)), reads=[PB[1]], writes=[WTnb])
            for h in range(8):
                P.add("pe", lambda e, h=h, WTn=WTn, Sc=Sc: e.matmul(PS[2][0:64, h * 64:(h + 1) * 64], lhsT=WTn[:, h, :], rhs=Sc[:, h, :], start=True, stop=True),
                      reads=[WTnb, Scb], writes=[PB[2]])
            P.add("dve", lambda e, Sn=Sn, Sc=Sc: e.tensor_tensor(out=Sn, in0=v3(2), in1=Sc, op=ALU.add), reads=[PB[2], Scb], writes=[Snb])
            Wc, Wcb = Wn, Wnb
            WTc, WTcb = WTn, WTnb
            Sc, Scb = Sn, Snb
        return q

    def state_step(d, j, ck, q, jo):
        cs = slice(ck * CH, (ck + 1) * CH)
        for h in range(8):
            P.add("pe", lambda e, h=h: e.matmul(PS[4][0:64, h * 64:(h + 1) * 64], lhsT=opnd(RH[j], h, ck, 0, 64), rhs=Hb16[:, h, :], start=True, stop=False),
                  reads=[RHb[j], Hbb], writes=[PB[4]])
            P.add("pe", lambda e, h=h: e.matmul(PS[4][0:64, h * 64:(h + 1) * 64], lhsT=ATk[q][:, h, 0:64], rhs=V0[j][:, ck, h, :], start=False, stop=True),
                  reads=[ATkb[q], V0b[j]], writes=[PB[4]])
        P.add("act", lambda e: e.activation(out=RS, in_=v3(4), func=AF.Copy), reads=[PB[4]], writes=[RSb])
        for h in range(8):
            P.add("pe", lambda e, h=h: e.matmul(PS[5][0:64, h * 64:(h + 1) * 64], lhsT=Sf[q][:, h, :], rhs=RS[:, h, :], start=True, stop=True),
                  reads=[Sfb[q], RSb], writes=[PB[5]])
        P.add("dve", lambda e: e.tensor_copy(out=U, in_=v3(5)), reads=[PB[5]], writes=[Ub])
        for h in range(8):
            P.add("pe", lambda e, h=h: e.matmul(PS[6][0:64, h * 64:(h + 1) * 64], lhsT=Hb16[:, h, :], rhs=opnd(RH[j], h, ck, 64, 128), start=True, stop=False),
                  reads=[RHb[j], Hbb], writes=[PB[6]])
            P.add("pe", lambda e, h=h: e.matmul(PS[6][0:64, h * 64:(h + 1) * 64], lhsT=U[:, h, :], rhs=ATb[q][:, h, 64:128], start=False, stop=False),
                  reads=[ATbb[q], Ub], writes=[PB[6]])
            P.add("pe", lambda e, h=h: e.matmul(PS[6][0:64, h * 64:(h + 1) * 64], lhsT=V0[j][:, ck, h, :], rhs=ATk[q][:, h, 64:128], start=False, stop=True),
                  reads=[ATkb[q], V0b[j]], writes=[PB[6]])
        for h in range(8):
            P.add("pe", lambda e, h=h: e.matmul(PS[7][0:64, h * 64:(h + 1) * 64], lhsT=BKT[q][:, 0, h, :], rhs=U[:, h, :], start=True, stop=False),
                  reads=[BKTb[q], Ub], writes=[PB[7]])
            P.add("pe", lambda e, h=h: e.matmul(PS[7][0:64, h * 64:(h + 1) * 64], lhsT=BKT[q][:, 1, h, :], rhs=V0[j][:, ck, h, :], start=False, stop=True),
                  reads=[BKTb[q], V0b[j]], writes=[PB[7]])
        P.add("dve", lambda e: e.tensor_tensor(out=Ht, in0=v3(7), in1=H, op=ALU.add), reads=[PB[7], Hbuf], writes=[Htb])
        P.add("dve", lambda e: e.tensor_tensor(out=H, in0=Ht, in1=GCt[j][:, :, ck:ck + 1].to_broadcast([64, 8, 64]), op=ALU.mult),
              reads=[Htb, GCb[j]], writes=[Hbuf])
        P.add("act", lambda e: e.activation(out=Hb16, in_=H, func=AF.Copy), reads=[Hbuf], writes=[Hbb])
        P.add("act", lambda e: e.activation(out=ost[jo][:, :, cs], in_=v3(6), func=AF.Copy), reads=[PB[6]], writes=[ostb[jo]])

    oc = 0
    for d in range(2):
        for s_ in range(nseq):
            P.add("dve", lambda e: e.memset(H, 0.0), writes=[Hbuf])
            P.add("dve", lambda e: e.memset(Hb16, 0.0), writes=[Hbb])
            tiles = list(range(ntile_seq)) if d == 0 else list(range(ntile_seq - 1, -1, -1))
            cks = list(range(NCK)) if d == 0 else list(range(NCK - 1, -1, -1))
            seq = []
            for ti in tiles:
                for ck in cks:
                    seq.append((ti, ck))
            loaded = {}
            pend = None

            def flush(pend):
                pj, pck, pq, pjo, pt0, plast = pend
                state_step(d, pj, pck, pq, pjo)
                if plast:
                    P.add("act", lambda e, pjo=pjo, pt0=pt0, d=d: e.dma_start(out=D.O[1][d].rearrange("(h i) t -> i h t", i=64)[:, :, pt0:pt0 + TT], in_=ost[pjo]),
                          reads=[ostb[pjo]], writes=[D.O[0]], lane=tag + "sto%d" % pjo)
            for n, (ti, ck) in enumerate(seq):
                if ti not in loaded:
                    loaded[ti] = load_tile(d, s_, ti)
                j, t0 = loaded[ti]
                q = precompute(d, j, ck)
                if pend is not None:
                    flush(pend)
                jo = (oc // NCK) % 2
                oc += 1
                pend = (j, ck, q, jo, t0, ck == cks[-1])
            flush(pend)
    P.barrier()
    ar.pop()


def pass_mix_out(C, P, ar, D, W, l, dst, nseq, T, tag):
    nc = C.nc
    ar.push()
    NTOK = nseq * T
    PS, PB = C.psum, C.psumb
    wo = ar.bf16(8 * D_MODEL).rearrange("p (k n) -> p k n", k=8)
    wob = Buf(tag + "wo", multi=True)
    for k in range(8):
        P.add("pool", lambda e, k=k: e.dma_start(out=wo[:, k, :], in_=W["w_out"][l][k * 128:(k + 1) * 128, :]), writes=[wob], lane="wgu%d" % (k % 4))
    gng, gngb = load_cols(C, P, ar, W["rw_gn_g"][l], 4, tag + "gng")
    gnb_, gnbb = load_cols(C, P, ar, W["rw_gn_b"][l], 4, tag + "gnb")
    lng, lnb = load_ln_params(C, P, ar, W["ln_g"], W["ln_b"], l, 1, tag + "ln")
    bones = ar.bf16(128)
    bonesb = Buf(tag + "bones")
    P.add("dve", lambda e: e.memset(bones, 0.0), writes=[bonesb])
    P.add("dve", lambda e: e.memset(bones[0:64, 0:64], 1.0), writes=[bonesb])
    P.add("dve", lambda e: e.memset(bones[64:128, 64:128], 1.0), writes=[bonesb])

    def f3(n=4):
        return ar.f32(n * TT).rearrange("p (c t) -> p c t", c=n)

    def b3(n=4):
        return ar.bf16(n * TT).rearrange("p (c t) -> p c t", c=n)
    o0 = f3(); o1 = f3(); bon = f3(); gt = f3()
    inb = Buf(tag + "in")
    o0b = Buf(tag + "o0"); o1b = Buf(tag + "o1"); bonb = Buf(tag + "bon"); gtb = Buf(tag + "gt")
    yda = b3(); ydab = Buf(tag + "yda")
    x1 = f3(8); x1b = Buf(tag + "x1")
    ob16 = b3(); ob16b = Buf(tag + "ob16")
    oq16 = b3(); oq16b = Buf(tag + "oq16")
    mean = f3(); meanb = Buf(tag + "mean")
    rstd = f3(); rstdb = Buf(tag + "rstd")
    yrw = b3(); yrwb = Buf(tag + "yrw")
    zb = b3(8); zq = b3(8)
    mr = ar.f32(2 * TT).rearrange("p (c t) -> p c t", c=2)
    tmp = {"zb": (zb, Buf(tag + "zb")), "zq": (zq, Buf(tag + "zq")), "ps": (PS[6], PB[6]), "mr": (mr, Buf(tag + "mr"))}
    dst_buf, dst_ap = dst
    ntiles = NTOK // TT
    fm4 = lambda ap, t0: ap.rearrange("(c p) t -> p c t", p=128)[:, :, t0:t0 + TT]
    for t in range(ntiles):
        t0 = t * TT
        P.add("sp", lambda e, t0=t0: e.dma_start(out=o0, in_=fm4(D.O[1][0], t0)), reads=[D.O[0]], writes=[o0b], lane=tag + "ld0")
        P.add("sp", lambda e, t0=t0: e.dma_start(out=o1, in_=fm4(D.O[1][1], t0)), reads=[D.O[0]], writes=[o1b], lane=tag + "ld1")
        P.add("sp", lambda e, t0=t0: e.dma_start(out=bon, in_=fm4(D.BON[1], t0)), reads=[D.BON[0]], writes=[bonb], lane=tag + "ld2")
        P.add("sp", lambda e, t0=t0: e.dma_start(out=gt, in_=fm4(D.GT[1], t0)), reads=[D.GT[0]], writes=[gtb], lane=tag + "ld3")
        P.add("sp", lambda e, t0=t0: e.dma_start(out=yda, in_=fm4(D.YDA[1], t0)), reads=[D.YDA[0]], writes=[ydab], lane=tag + "ld4")
        P.add("sp", lambda e, t0=t0: e.dma_start(out=x1, in_=fm4(D.X1[1], t0)), reads=[D.X1[0]], writes=[x1b], lane=tag + "ld5")
        P.add("dve", lambda e: e.tensor_tensor(out=o0, in0=o0, in1=o1, op=ALU.add), reads=[o0b, o1b], writes=[o0b])
        P.add("act", lambda e: e.activation(out=ob16, in_=o0, func=AF.Copy), reads=[o0b], writes=[ob16b])
        P.add("dve", lambda e: e.tensor_tensor(out=oq16, in0=o0, in1=o0, op=ALU.mult), reads=[o0b], writes=[oq16b])
        for c in range(4):
            bk = c // 2
            P.add("pe", lambda e, c=c, bk=bk: e.matmul(PS[bk][:, (c % 2) * TT:(c % 2 + 1) * TT], lhsT=bones, rhs=ob16[:, c, :], start=True, stop=True),
                  reads=[bonesb, ob16b], writes=[PB[bk]])
        for c in range(4):
            bk = 2 + c // 2
            P.add("pe", lambda e, c=c, bk=bk: e.matmul(PS[bk][:, (c % 2) * TT:(c % 2 + 1) * TT], lhsT=bones, rhs=oq16[:, c, :], start=True, stop=True),
                  reads=[bonesb, oq16b], writes=[PB[bk]])
        for hb in range(2):
            v2 = lambda bk: PS[bk][:, 0:2 * TT].rearrange("p (c t) -> p c t", c=2)
            mv = mean[:, 2 * hb:2 * hb + 2, :]
            rv = rstd[:, 2 * hb:2 * hb + 2, :]
            P.add("dve", lambda e, hb=hb, mv=mv: e.tensor_scalar(out=mv, in0=PS[hb][:, 0:2 * TT].rearrange("p (c t) -> p c t", c=2), scalar1=1.0 / 64, scalar2=None, op0=ALU.mult),
                  reads=[PB[hb]], writes=[meanb])
            P.add("dve", lambda e, mv=mv, rv=rv: e.tensor_tensor(out=rv, in0=mv, in1=mv, op=ALU.mult), reads=[meanb], writes=[rstdb])
            P.add("dve", lambda e, hb=hb, rv=rv: e.scalar_tensor_tensor(out=rv, in0=PS[2 + hb][:, 0:2 * TT].rearrange("p (c t) -> p c t", c=2), scalar=1.0 / 64, in1=rv,
                                                                       op0=ALU.mult, op1=ALU.subtract), reads=[PB[2 + hb], rstdb], writes=[rstdb])
        P.add("act", lambda e: e.activation(out=rstd, in_=rstd, func=AF.Sqrt, bias=GN_EPS, scale=1.0), reads=[rstdb], writes=[rstdb])
        P.add("dve", lambda e: e.reciprocal(out=rstd, in_=rstd), reads=[rstdb], writes=[rstdb])
        P.add("dve", lambda e: e.tensor_tensor(out=o0, in0=o0, in1=mean, op=ALU.subtract), reads=[o0b, meanb], writes=[o0b])
        P.add("dve", lambda e: e.tensor_tensor(out=o0, in0=o0, in1=rstd, op=ALU.mult), reads=[o0b, rstdb], writes=[o0b])
        for c in range(4):
            P.add("act", lambda e, c=c: e.activation(out=o0[:, c, :], in_=o0[:, c, :], func=AF.Identity, scale=gng[:, c:c + 1], bias=gnb_[:, c:c + 1]),
                  reads=[o0b, gngb, gnbb], writes=[o0b])
        P.add("dve", lambda e: e.tensor_tensor(out=o0, in0=o0, in1=bon, op=ALU.add), reads=[o0b, bonb], writes=[o0b])
        P.add("dve", lambda e: e.tensor_tensor(out=yrw, in0=o0, in1=gt, op=ALU.mult), reads=[o0b, gtb], writes=[yrwb])
        for dch in range(8):
            bk = 4 + dch % 2
            for k in range(8):
                rhs = yrw[:, k, :] if k < 4 else yda[:, k - 4, :]
                P.add("pe", lambda e, k=k, dch=dch, bk=bk, rhs=rhs: e.matmul(PS[bk][:, 0:TT], lhsT=wo[:, k, dch * 128:(dch + 1) * 128], rhs=rhs,
                                                                              start=(k == 0), stop=(k == 7)), reads=[wob, yrwb, ydab], writes=[PB[bk]])
            P.add("dve", lambda e, dch=dch, bk=bk: e.scalar_tensor_tensor(out=x1[:, dch, :], in0=PS[bk][:, 0:TT], scalar=1.0 / ALPHA, in1=x1[:, dch, :],
                                                                          op0=ALU.mult, op1=ALU.add), reads=[PB[bk], x1b], writes=[x1b])
        layer_norm_fm(C, P, x1, x1b, TT, lng, lnb, x1, x1b, tmp, eps=LN_EPS / (ALPHA * ALPHA))
        P.add("act", lambda e, t0=t0: e.dma_start(out=fm4(dst_ap, t0), in_=x1), reads=[x1b], writes=[dst_buf], lane=tag + "stx")
    P.barrier()
    ar.pop()


def rope_table(T):
    n = np.arange(64)
    inv = np.power(np.float32(ROPE_THETA), -(np.arange(0, 64, 2, dtype=np.float32)) / np.float32(64)).astype(np.float32)
    ang = (np.arange(T, dtype=np.float32)[None, :] * inv[n % 32][:, None]).astype(np.float32)
    cos = np.cos(ang.astype(np.float64)).astype(np.float32)
    sin = np.sin(ang.astype(np.float64)).astype(np.float32)
    sign = np.where(n < 32, -1.0, 1.0).astype(np.float32)[:, None]
    tab = np.stack([np.concatenate([cos, cos], 0), np.concatenate([sin * sign, sin * sign], 0)], 0)
    return np.ascontiguousarray(tab.astype(np.float32))


def make_consts(T):
    c = {}
    c["c_ident"] = np.eye(128, dtype=np.float32)
    c["c_rope"] = rope_table(T)
    c["c_rwmask"] = rw_masks()
    return c


PARAM_SHAPES = {
    "ln_g": [DEPTH, 3, D_MODEL], "ln_b": [DEPTH, 3, D_MODEL],
    "ffn_w_gu": [DEPTH, 2, D_MODEL, 2 * D_FF], "ffn_w_down": [DEPTH, 2, D_FF, D_MODEL],
    "w_in": [DEPTH, D_MODEL, N_IN], "w_out": [DEPTH, D_MODEL, D_MODEL],
    "shift_mu": [DEPTH, 2, RW_COLS], "rw_w_up": [DEPTH, 2, 64, RW_W], "rw_w_bias": [DEPTH, 2, RW_W],
    "rw_a_up": [DEPTH, 2, 64, RW_W], "rw_a_bias": [DEPTH, 2, RW_W], "rw_g_up": [DEPTH, 128, RW_W],
    "rw_k_k": [DEPTH, RW_W], "rw_k_a": [DEPTH, RW_W], "rw_r_k": [DEPTH, 8, 64],
    "rw_gn_g": [DEPTH, RW_W], "rw_gn_b": [DEPTH, RW_W], "da_lambda": [DEPTH, 4, 64], "da_subln_g": [DEPTH, 128],
}


def build_program(nseq, T, mode="full", debug_outs=(), depth=DEPTH):
    NTOK = nseq * T
    nc = bass.Bass("TRN2", target_bir_lowering=False)
    es = ExitStack()
    C = Ctx()
    C.nc = nc
    P = Prog()

    def din(name, shape):
        return nc.dram_tensor(name, list(shape), F32, kind="ExternalInput").ap()

    x_in = din("x", [NTOK, D_MODEL])
    W = {k: din(k, [depth] + list(v[1:])) for k, v in PARAM_SHAPES.items()}
    lamc_in = din("c_lam", [128, 2])
    ident_in = din("c_ident", [128, 128])
    rope_in = din("c_rope", [2, 128, T])
    mask_in = din("c_rwmask", [2, 64, 192])
    y_out = nc.dram_tensor("y", [NTOK, D_MODEL], F32, kind="ExternalOutput").ap()

    D = Ctx()

    def scratch(name, shape, dt):
        kind = "ExternalOutput" if name in debug_outs else "Internal"
        ap = nc.dram_tensor("s_" + name, list(shape), dt, kind=kind).ap()
        setattr(D, name, (Buf("s_" + name, multi=True), ap))

    scratch("XA", [D_MODEL, NTOK], F32)
    scratch("X1", [D_MODEL, NTOK], F32)
    scratch("PRW", [RW_COLS, NTOK], F32)
    scratch("QT", [512, NTOK], BF16)
    scratch("KT", [512, NTOK], BF16)
    scratch("VT", [NTOK, 512], BF16)
    scratch("YDA", [512, NTOK], BF16)
    scratch("OPS", [8, 512, NTOK], BF16)
    scratch("VTOK", [NTOK, 512], BF16)
    scratch("GC", [2, 512, NTOK // 64], F32)
    scratch("GT", [512, NTOK], F32)
    scratch("BON", [512, NTOK], F32)
    scratch("O", [2, 512, NTOK], F32)
    scratch("YRW", [512, NTOK], BF16)

    SB_WORDS = 47 * 1024
    arena_t = es.enter_context(nc.sbuf_tensor("arena", [128, SB_WORDS], F32))
    ar = Arena(arena_t[:], SB_WORDS)
    C.psum = []
    C.psumb = []
    for i in range(8):
        pt = es.enter_context(nc.psum_tensor("psum%d" % i, [128, 512], F32))
        C.psum.append(pt[:])
        C.psumb.append(Buf("psum%d" % i))

    P.bar_ap = ar.f32(1)
    C.constb = Buf("const")
    C.ident_f32 = ar.f32(128)
    C.ones_bf = ar.bf16(128)
    C.ident_bf = ar.bf16(128)
    P.add("sp", lambda e: e.dma_start(out=C.ident_f32, in_=ident_in[:, :]), writes=[C.constb], lane="const")
    P.add("dve", lambda e: e.memset(C.ones_bf, 1.0), writes=[C.constb])
    P.add("dve", lambda e: e.tensor_copy(out=C.ident_bf, in_=C.ident_f32), writes=[C.constb])

    xin_b = Buf("x_in", multi=True)
    yout_b = Buf("y_out", multi=True)

    if mode == "ffn_only":
        pass_ffn(C, P, ar, (xin_b, x_in), "tok", (yout_b, y_out), "tok",
                 W["ffn_w_gu"][0, 0], W["ffn_w_down"][0, 0], W["ln_g"], W["ln_b"], 0, 0, NTOK, "f00")
    elif mode == "proj_attn1":
        pass_ffn(C, P, ar, (xin_b, x_in), "tok", D.X1, "fm",
                 W["ffn_w_gu"][1, 0], W["ffn_w_down"][1, 0], W["ln_g"], W["ln_b"], 1, 0, NTOK, "f00")
        pass_proj(C, P, ar, D, W["w_in"][1], rope_in, nseq, T, "pj0")
        pass_attn(C, P, ar, D, W["da_lambda"], W["da_subln_g"], 1, 0.8 - 0.6 * math.exp(-0.3 * 1), nseq, T, "at0")
    elif mode == "proj_attn":
        pass_ffn(C, P, ar, (xin_b, x_in), "tok", D.X1, "fm",
                 W["ffn_w_gu"][0, 0], W["ffn_w_down"][0, 0], W["ln_g"], W["ln_b"], 0, 0, NTOK, "f00")
        pass_proj(C, P, ar, D, W["w_in"][0], rope_in, nseq, T, "pj0")
        pass_attn(C, P, ar, D, W["da_lambda"], W["da_subln_g"], 0, 0.8 - 0.6 * math.exp(-0.3 * 0), nseq, T, "at0")
    elif mode == "rwkv":
        pass_ffn(C, P, ar, (xin_b, x_in), "tok", D.X1, "fm",
                 W["ffn_w_gu"][0, 0], W["ffn_w_down"][0, 0], W["ln_g"], W["ln_b"], 0, 0, NTOK, "f00")
        pass_proj(C, P, ar, D, W["w_in"][0], rope_in, nseq, T, "pj0")
        pass_rwkv_prep(C, P, ar, D, W, 0, nseq, T, "rp0")
        pass_rwkv_scan(C, P, ar, D, mask_in, nseq, T, "rs0")
    elif mode == "full":
        scratch("XB", [D_MODEL, NTOK], F32)
        src, skind = (xin_b, x_in), "tok"
        for l in range(depth):
            lam_init = 0.8 - 0.6 * math.exp(-0.3 * l)
            pass_ffn(C, P, ar, src, skind, D.X1, "fm", W["ffn_w_gu"][l, 0], W["ffn_w_down"][l, 0], W["ln_g"], W["ln_b"], l, 0, NTOK, "fa%d" % l)
            pass_proj(C, P, ar, D, W["w_in"][l], rope_in, nseq, T, "pj%d" % l)
            pass_attn(C, P, ar, D, W["da_lambda"], W["da_subln_g"], l, lam_init, nseq, T, "at%d" % l, lamc_ap=(lamc_in if depth == 1 else None))
            pass_rwkv_prep(C, P, ar, D, W, l, nseq, T, "rp%d" % l)
            pass_rwkv_scan(C, P, ar, D, mask_in, nseq, T, "rs%d" % l)
            pass_mix_out(C, P, ar, D, W, l, D.XA, nseq, T, "mo%d" % l)
            if l == depth - 1:
                pass_ffn(C, P, ar, D.XA, "fm", (yout_b, y_out), "tok", W["ffn_w_gu"][l, 1], W["ffn_w_down"][l, 1], W["ln_g"], W["ln_b"], l, 2, NTOK, "fb%d" % l)
            else:
                pass_ffn(C, P, ar, D.XA, "fm", D.XB, "fm", W["ffn_w_gu"][l, 1], W["ffn_w_down"][l, 1], W["ln_g"], W["ln_b"], l, 2, NTOK, "fb%d" % l)
                src, skind = D.XB, "fm"
    else:
        raise NotImplementedError(mode)

    P.emit(nc, es)
    es.close()
    return nc


_PROG_CACHE = {}

N_CORES = 8
SEQ_LEN = 8192
NSEQ_CORE = 2


def kernel(**inputs):
    T = SEQ_LEN
    key = (1, T, 1)
    if key not in _PROG_CACHE:
        _PROG_CACHE[key] = build_program(1, T, mode="full", depth=1)
    nc = _PROG_CACHE[key]
    xp = np.asarray(inputs["x_prompt"], dtype=np.float32)
    xsm = np.asarray(inputs["x_sample"], dtype=np.float32)
    consts = make_consts(T)
    nsm = xsm.shape[0]
    groups = [[np.ascontiguousarray(xp[c]) for c in range(N_CORES)],
              [np.ascontiguousarray(xsm[c % nsm]) for c in range(N_CORES)]]
    for l in range(DEPTH):
        lam_init = 0.8 - 0.6 * math.exp(-0.3 * l)
        params = {k: np.ascontiguousarray(np.asarray(inputs[k], dtype=np.float32)[l:l + 1]) for k in PARAM_SHAPES}
        lamc = np.empty((128, 2), dtype=np.float32)
        lamc[:, 0] = -lam_init
        lamc[:, 1] = 1.0 - lam_init
        for g in range(2):
            in_maps = []
            for c in range(N_CORES):
                m = {"x": groups[g][c], "c_lam": lamc}
                m.update(params)
                m.update(consts)
                in_maps.append(m)
            res = run_bass_kernel_spmd(nc, in_maps, core_ids=list(range(N_CORES)))
            groups[g] = [np.ascontiguousarray(res.results[c]["y"]) for c in range(N_CORES)]
    y_prompt = np.stack(groups[0], axis=0)
    y_sample = np.stack(groups[1][:nsm], axis=0)
    return (y_prompt.astype(np.float32), y_sample.astype(np.float32))
```
